# Optimizing a Trainium2 kernel written in Bass

```python
import math
import jax, jax.numpy as jnp
from jax import lax
import numpy as np

D_MODEL = 1024
BATCH = 16
SEQ = 4096
DEPTH = 4

N_MIXERS = 2
N_ATTN_LAYERS = (DEPTH + 1) // 2
N_HGRN_LAYERS = DEPTH // 2

ATTN_HEADS = 8
ATTN_HEAD_DIM = 64
ATTN_V_DIM = 2 * ATTN_HEAD_DIM
ROPE_THETA = 10000.0
Q_BLOCK = 128
SUBLN_EPS = 1e-5
LAMBDA_STD = 0.1

HG_HEADS = 8
HG_KEY_DIM = D_MODEL // HG_HEADS
HG_VAL_DIM = D_MODEL // HG_HEADS
CHUNK = 16
GNORM_EPS = 1e-6

FFN_HIDDEN = ((8 * D_MODEL // 3 + 255) // 256) * 256
NORM_EPS = 1e-6
MAX_POS_OFFSET = 1024

kernel_name = "interleaved_diffattn_hgrn2_swiglu"

F32 = jnp.float32


def rms_norm(x, w, eps=NORM_EPS):
    xf = x.astype(F32)
    y = xf * lax.rsqrt(jnp.mean(xf * xf, axis=-1, keepdims=True) + eps)
    return (y * w.astype(F32)).astype(x.dtype)


def rope_tables(positions):
    inv_freq = 1.0 / (ROPE_THETA ** (jnp.arange(0, ATTN_HEAD_DIM, 2, dtype=F32) / ATTN_HEAD_DIM))
    ang = positions.astype(F32)[..., None] * inv_freq
    return jnp.cos(ang), jnp.sin(ang)


def apply_rope(t, cos, sin):
    tf = t.astype(F32)
    t1, t2 = jnp.split(tf, 2, axis=-1)
    c = cos[:, :, None, None, :]
    s = sin[:, :, None, None, :]
    return jnp.concatenate([t1 * c - t2 * s, t2 * c + t1 * s], axis=-1).astype(t.dtype)


def lambda_init_fn(layer_idx):
    return 0.8 - 0.6 * math.exp(-0.3 * layer_idx)


def diff_attention(h, cos, sin, w_in, w_out, lq1, lk1, lq2, lk2, subln_w, lambda_init):
    B, S, _ = h.shape
    qkv = h @ w_in
    q, k, v = jnp.split(qkv, [D_MODEL, 2 * D_MODEL], axis=-1)
    q = apply_rope(q.reshape(B, S, ATTN_HEADS, 2, ATTN_HEAD_DIM), cos, sin)
    k = apply_rope(k.reshape(B, S, ATTN_HEADS, 2, ATTN_HEAD_DIM), cos, sin)
    v = v.reshape(B, S, ATTN_HEADS, ATTN_V_DIM)
    lam = (jnp.exp(jnp.sum(lq1.astype(F32) * lk1.astype(F32)))
           - jnp.exp(jnp.sum(lq2.astype(F32) * lk2.astype(F32))) + lambda_init)
    scale = ATTN_HEAD_DIM ** -0.5
    n_blk = S // Q_BLOCK
    q_blocks = q.reshape(B, n_blk, Q_BLOCK, ATTN_HEADS, 2, ATTN_HEAD_DIM).transpose(1, 0, 2, 3, 4, 5)
    k_idx = jnp.arange(S)

    def block(args):
        qb, blk = args
        s = jnp.einsum('bqhcd,bkhcd->bhcqk', qb, k).astype(F32) * scale
        q_idx = blk * Q_BLOCK + jnp.arange(Q_BLOCK)
        causal = k_idx[None, :] <= q_idx[:, None]
        s = jnp.where(causal, s, jnp.finfo(F32).min)
        p = jax.nn.softmax(s, axis=-1)
        a = p[:, :, 0] - lam * p[:, :, 1]
        return jnp.einsum('bhqk,bkhe->bqhe', a.astype(v.dtype), v)

    o = lax.map(block, (q_blocks, jnp.arange(n_blk)))
    o = o.transpose(1, 0, 2, 3, 4).reshape(B, S, ATTN_HEADS, ATTN_V_DIM)
    o = rms_norm(o, subln_w, SUBLN_EPS).astype(F32) * (1.0 - lambda_init)
    return o.reshape(B, S, D_MODEL).astype(h.dtype) @ w_out


def hgrn2_mixer(h, w_in, w_out, gnorm_w, lb):
    B, S, _ = h.shape
    n_c = S // CHUNK
    q, fz, i, g = jnp.split(h @ w_in, 4, axis=-1)
    f = lb.astype(F32) + (1.0 - lb.astype(F32)) * jax.nn.sigmoid(fz.astype(F32))
    k = 1.0 - f
    logf = jnp.log(f)

    def to_chunks(t):
        return t.reshape(B, n_c, CHUNK, HG_HEADS, -1).transpose(0, 3, 1, 2, 4)

    qc = to_chunks(jax.nn.silu(q.astype(F32)))
    kc = to_chunks(k)
    vc = to_chunks(i.astype(F32))
    bc = jnp.cumsum(to_chunks(logf), axis=3)

    qd = qc * jnp.exp(bc)
    kd = kc * jnp.exp(-bc)
    causal = jnp.tril(jnp.ones((CHUNK, CHUNK), dtype=bool))
    A = jnp.where(causal, jnp.einsum('bhncK,bhnjK->bhncj', qd, kd), 0.0)
    o_intra = jnp.einsum('bhncj,bhnjv->bhncv', A, vc)

    b_last = bc[:, :, :, -1:, :]
    k_to_end = kc * jnp.exp(b_last - bc)
    chunk_decay = jnp.exp(b_last[:, :, :, 0, :])

    def step(state, xs):
        qd_n, kend_n, v_n, dec_n = xs
        o_n = jnp.einsum('bhcK,bhKv->bhcv', qd_n, state)
        state = dec_n[..., None] * state + jnp.einsum('bhcK,bhcv->bhKv', kend_n, v_n)
        return state, o_n

    xs = (jnp.moveaxis(qd, 2, 0), jnp.moveaxis(k_to_end, 2, 0),
          jnp.moveaxis(vc, 2, 0), jnp.moveaxis(chunk_decay, 2, 0))
    state0 = jnp.zeros((B, HG_HEADS, HG_KEY_DIM, HG_VAL_DIM), F32)
    _, o_inter = lax.scan(step, state0, xs)
    o = o_intra + jnp.moveaxis(o_inter, 0, 2)
    o = o.transpose(0, 2, 3, 1, 4).reshape(B, S, HG_HEADS, HG_VAL_DIM)
    o = rms_norm(o, gnorm_w, GNORM_EPS) * jax.nn.silu(g.astype(F32)).reshape(B, S, HG_HEADS, HG_VAL_DIM)
    return o.reshape(B, S, D_MODEL).astype(h.dtype) @ w_out


def swiglu(h, w_in, w_out):
    gate, up = jnp.split(h @ w_in, 2, axis=-1)
    return (jax.nn.silu(gate) * up) @ w_out


def setup_inputs(seed: int = 0) -> dict:
    key = jax.random.key(seed)
    ks = jax.random.split(key, 20)
    D = D_MODEL
    nrm = lambda k, shape, s: jax.random.normal(k, shape, F32) * s
    x = jax.random.normal(ks[0], (BATCH, SEQ, D), F32)
    start = jax.random.randint(ks[1], (BATCH, 1), 0, MAX_POS_OFFSET, dtype=jnp.int32)
    positions = (start + jnp.arange(SEQ, dtype=jnp.int32)[None, :]).astype(jnp.int32)
    return {
        "x": x,
        "positions": positions,
        "norm_mix_w": 1.0 + nrm(ks[2], (DEPTH, D), 0.02),
        "norm_ffn_w": 1.0 + nrm(ks[3], (DEPTH, D), 0.02),
        "final_norm_w": 1.0 + nrm(ks[4], (D,), 0.02),
        "attn_w_in": nrm(ks[5], (N_ATTN_LAYERS, D, 3 * D), D ** -0.5),
        "attn_w_out": nrm(ks[6], (N_ATTN_LAYERS, D, D), D ** -0.5),
        "attn_lambda_q1": nrm(ks[7], (N_ATTN_LAYERS, ATTN_HEAD_DIM), LAMBDA_STD),
        "attn_lambda_k1": nrm(ks[8], (N_ATTN_LAYERS, ATTN_HEAD_DIM), LAMBDA_STD),
        "attn_lambda_q2": nrm(ks[9], (N_ATTN_LAYERS, ATTN_HEAD_DIM), LAMBDA_STD),
        "attn_lambda_k2": nrm(ks[10], (N_ATTN_LAYERS, ATTN_HEAD_DIM), LAMBDA_STD),
        "attn_subln_w": 1.0 + nrm(ks[11], (N_ATTN_LAYERS, ATTN_V_DIM), 0.02),
        "hgrn_w_in": nrm(ks[12], (N_HGRN_LAYERS, D, 4 * D), D ** -0.5),
        "hgrn_w_out": nrm(ks[13], (N_HGRN_LAYERS, D, D), D ** -0.5),
        "hgrn_gnorm_w": 1.0 + nrm(ks[14], (N_HGRN_LAYERS, HG_VAL_DIM), 0.02),
        "hgrn_lb_param": nrm(ks[15], (DEPTH, D), 0.1),
        "ffn_w_in": nrm(ks[16], (DEPTH, D, 2 * FFN_HIDDEN), D ** -0.5),
        "ffn_w_out": nrm(ks[17], (DEPTH, FFN_HIDDEN, D), FFN_HIDDEN ** -0.5),
    }


def reference(x, positions, norm_mix_w, norm_ffn_w, final_norm_w,
              attn_w_in, attn_w_out, attn_lambda_q1, attn_lambda_k1, attn_lambda_q2, attn_lambda_k2,
              attn_subln_w, hgrn_w_in, hgrn_w_out, hgrn_gnorm_w, hgrn_lb_param,
              ffn_w_in, ffn_w_out):
    cos, sin = rope_tables(positions)
    lbs = jnp.cumsum(jax.nn.softmax(hgrn_lb_param.astype(F32), axis=0), axis=0)
    lbs = lbs - lbs[0:1]
    for i in range(DEPTH):
        h = rms_norm(x, norm_mix_w[i])
        j = i // N_MIXERS
        if i % N_MIXERS == 0:
            y = diff_attention(h, cos, sin, attn_w_in[j], attn_w_out[j],
                               attn_lambda_q1[j], attn_lambda_k1[j], attn_lambda_q2[j], attn_lambda_k2[j],
                               attn_subln_w[j], lambda_init_fn(i))
        else:
            y = hgrn2_mixer(h, hgrn_w_in[j], hgrn_w_out[j], hgrn_gnorm_w[j], lbs[i])
        x = x + y.astype(x.dtype)
        x = x + swiglu(rms_norm(x, norm_ffn_w[i]), ffn_w_in[i], ffn_w_out[i]).astype(x.dtype)
    return rms_norm(x, final_norm_w)
```

```python
import math
from contextlib import ExitStack

import numpy as np
import concourse.bass as bass
import concourse.mybir as mybir
from concourse.bass_utils import run_bass_kernel_spmd

F32 = mybir.dt.float32
BF16 = mybir.dt.bfloat16
I32 = mybir.dt.int32
AF = mybir.ActivationFunctionType
ALU = mybir.AluOpType
AX = mybir.AxisListType

D = 1024
FH = 2816
NH = 8
SEM_LIMIT = 30000
NORM_EPS = 1e-6
SUBLN_EPS = 1e-5
GN_EPS = 1e-6


class Counter:
    def __init__(self, S):
        self.S = S
        self.sem = None
        self.val = 0

    def peek(self):
        return (self.sem, self.val)

    def next(self, inc):
        if self.sem is None or self.val + inc > SEM_LIMIT:
            self.sem = self.S.new_sem()
            self.val = 0
        self.val += inc
        return (self.sem, self.val)


class Buf:
    __slots__ = ("name", "w", "r")

    def __init__(self, name=""):
        self.name = name
        self.w = []
        self.r = []


class Sched:
    def __init__(self, nc, es, same_engine_sync=("act", "dve", "pool")):
        self.nc = nc
        self.es = es
        self.eng = {"pe": nc.tensor, "act": nc.scalar, "dve": nc.vector,
                    "pool": nc.gpsimd, "sp": nc.sync}
        self.ctr = {e: Counter(self) for e in self.eng}
        self.waited = {e: {} for e in self.eng}
        self.same = set(same_engine_sync)
        self.nsem = 0
        self.rings = {}
        self.n_inst = {e: 0 for e in self.eng}
        self.n_wait = {e: 0 for e in self.eng}

    def new_sem(self):
        self.nsem += 1
        return self.es.enter_context(self.nc.semaphore(f"s{self.nsem}"))

    def _wait(self, E, tok):
        sem, val = tok
        if sem is None or val <= 0:
            return
        w = self.waited[E]
        k = id(sem)
        if w.get(k, 0) >= val:
            return
        w[k] = val
        self.eng[E].wait_ge(sem, val)
        self.n_wait[E] += 1

    def _deps(self, E, reads, writes):
        own = self.ctr[E].sem
        for b in reads:
            for t in b.w:
                if t[0] is own and E not in self.same:
                    continue
                self._wait(E, t)
        for b in writes:
            for t in b.w + b.r:
                if t[0] is own and E not in self.same:
                    continue
                self._wait(E, t)

    def _commit(self, tok, reads, writes):
        for b in reads:
            b.r.append(tok)
            if len(b.r) > 32:
                b.r = b.r[-32:]
        for b in writes:
            b.w = [tok]
            b.r = []

    def op(self, E, fn, reads=(), writes=()):
        self._deps(E, reads, writes)
        inst = fn(self.eng[E])
        tok = self.ctr[E].next(1)
        inst.then_inc(tok[0], 1)
        self.n_inst[E] += 1
        self._commit(tok, reads, writes)
        return tok

    def dma(self, Q, out, in_, reads=(), writes=(), ring="ld", nring=8):
        key = (Q, ring)
        if key not in self.rings:
            self.rings[key] = [[Counter(self) for _ in range(nring)], 0]
        rg = self.rings[key]
        c = rg[0][rg[1] % len(rg[0])]
        rg[1] += 1
        self._wait(Q, c.peek())
        self._deps(Q, reads, writes)
        tok = c.next(16)
        self.eng[Q].dma_start(out=out, in_=in_).then_inc(tok[0], 16)
        self.n_inst[Q] += 1
        self._commit(tok, reads, writes)
        return tok

    def barrier(self):
        toks = [c.peek() for c in self.ctr.values()]
        for rg in self.rings.values():
            toks += [c.peek() for c in rg[0]]
        for E in self.eng:
            for t in toks:
                self._wait(E, t)


class Tl:
    def __init__(self, t, n=1):
        self.t = t
        self.bs = [Buf() for _ in range(n)]

    @property
    def b(self):
        return self.bs[0]


class Ctx:
    pass


def sbt(K, es, name, shape, dt, n=1):
    K.uid += 1
    return Tl(es.enter_context(K.nc.sbuf_tensor(f"{name}_{K.uid}", shape, dt)), n)


def bank(K, i, n=1):
    return K.ps[:, i * 512:(i + n) * 512]


def next_pair(K):
    p = K.pair_i % 2
    K.pair_i += 1
    return p


def rstd_ops(K, ss, rs, n, inv_n, eps, rd, wr):
    S = K.S
    S.op("act", lambda e: e.activation(rs, ss, AF.Ln, bias=K.epsb[eps].t[:, 0:1], scale=inv_n), reads=rd + [K.epsb[eps].b], writes=wr)
    S.op("act", lambda e: e.activation(rs, rs, AF.Exp, scale=-0.5), reads=wr, writes=wr)


def norm_xs(K, xt, gw, junk, ss, xs):
    S = K.S
    S.op("act", lambda e: e.activation(junk.t[:], xt.t[:], AF.Square, accum_out=ss.t[:, 0:1]),
         reads=[xt.b], writes=[junk.b, ss.b])
    rstd_ops(K, ss.t[:, 0:1], ss.t[:, 1:2], 1, 1.0 / D, NORM_EPS, [ss.b], [ss.b])
    S.op("dve", lambda e: e.scalar_tensor_tensor(xs.t[:], xt.t[:], ss.t[:, 1:2], gw.t[:], ALU.mult, ALU.mult),
         reads=[xt.b, ss.b, gw.b], writes=[xs.b])


def transpose8(K, src, pbank_i, dst_ap, dst_bufs, eng="act", nblk=8):
    S = K.S
    pT = bank(K, pbank_i).bitcast(BF16)
    for k in range(nblk):
        S.op("pe", lambda e, k=k: e.transpose(pT[:, k * 128:(k + 1) * 128], src.t[:, k * 128:(k + 1) * 128], K.idb.t[:]),
             reads=[src.b, K.idb.b], writes=[K.pb[pbank_i]])
    pv = pT[:, 0:nblk * 128].rearrange("p (k t) -> p k t", k=nblk)
    if eng == "act":
        S.op("act", lambda e: e.copy(dst_ap, pv), reads=[K.pb[pbank_i]], writes=dst_bufs)
    else:
        S.op(eng, lambda e: e.tensor_copy(dst_ap, pv), reads=[K.pb[pbank_i]], writes=dst_bufs)


def load_w(K, wt, src, nk, per=1):
    v = src.rearrange("(k p) n -> p k n", p=128)
    for k0 in range(0, nk, per):
        k1 = min(nk, k0 + per)
        K.S.dma("pool", wt.t[:, k0:k1, :], v[:, k0:k1, :], writes=wt.bs[k0:k1], ring="w")


def silu_sig(K, src_ps, src_buf, ea, eb):
    S = K.S
    S.op("act", lambda e: e.activation(ea.t[:], src_ps, AF.Exp, scale=-1.0), reads=[src_buf], writes=[ea.b])
    S.op("act", lambda e: e.activation(eb.t[:], ea.t[:], AF.Ln, bias=K.epsb[1.0].t[:, 0:1], scale=1.0), reads=[ea.b, K.epsb[1.0].b], writes=[eb.b])
    S.op("act", lambda e: e.activation(ea.t[:], eb.t[:], AF.Exp, scale=-1.0), reads=[eb.b], writes=[ea.b])


def ffn_phase(K, l, src, dst, final):
    nc, S = K.nc, K.S
    NT = K.NT
    with ExitStack() as es:
        w1 = sbt(K, es, "w1", [128, 8, 2 * FH], BF16, 8)
        w2 = sbt(K, es, "w2", [128, 22, D], BF16, 22)
        gw = sbt(K, es, "gw", [128, D], F32)
        fwt = sbt(K, es, "fwt", [128, D], F32) if final else None
        xt = [sbt(K, es, f"xt{i}", [128, D], F32) for i in range(3)]
        ss = [sbt(K, es, f"ss{i}", [128, 2], F32) for i in range(2)]
        junk = sbt(K, es, "junk", [128, D], BF16)
        junk2 = junk
        xs = sbt(K, es, "xs", [128, D], BF16)
        xT = [sbt(K, es, f"xT{i}", [128, 8, 128], BF16) for i in range(2)]
        ea = [sbt(K, es, f"ea{i}", [128, 512], F32) for i in range(2)]
        eb = [sbt(K, es, f"eb{i}", [128, 512], F32) for i in range(2)]
        hh2 = [sbt(K, es, f"h{i}", [128, FH], BF16) for i in range(2)]
        hT = sbt(K, es, "hT", [128, 22, 128], BF16)
        xo = [sbt(K, es, f"xo{i}", [128, D], F32) for i in range(2)]
        xf = [sbt(K, es, f"xf{i}", [128, D], F32) for i in range(2)] if final else None
        ss2 = sbt(K, es, "ss2", [128, 2], F32)

        S.dma("sp", gw.t[:], K.norm_ffn_w[l].partition_broadcast(128), writes=[gw.b])
        if final:
            S.dma("sp", fwt.t[:], K.final_norm_w.partition_broadcast(128), writes=[fwt.b])
        load_w(K, w1, K.ffn_w_in[l], 8, 1)
        load_w(K, w2, K.ffn_w_out[l], 22, 4)

        def prologue(t):
            sl = t % 2
            x3 = xt[t % 3]
            S.dma("sp", x3.t[:], src[t * 128:(t + 1) * 128, :], writes=[x3.b])
            norm_xs(K, x3, gw, junk, ss[sl], xs)
            transpose8(K, xs, 4, xT[sl].t[:], [xT[sl].b], "act")

        def chunk(t, j):
            sl = t % 2
            h = hh2[t % 2]
            wd = 512 if j < 5 else 256
            p = j % 2
            G, U = bank(K, 2 * p)[:, 0:wd], bank(K, 2 * p + 1)[:, 0:wd]
            for which, pb_i, off in ((G, 2 * p, 0), (U, 2 * p + 1, FH)):
                for k in range(8):
                    S.op("pe", lambda e, k=k, which=which, off=off: e.matmul(
                        which, xT[sl].t[:, k, :], w1.t[:, k, off + j * 512: off + j * 512 + wd],
                        start=(k == 0), stop=(k == 7)),
                        reads=[xT[sl].b, w1.bs[k]], writes=[K.pb[pb_i]])
            A, Bq = ea[p], eb[p]
            S.op("act", lambda e: e.activation(A.t[:, 0:wd], G, AF.Exp, scale=-1.0), reads=[K.pb[2 * p]], writes=[A.b])
            S.op("act", lambda e: e.activation(Bq.t[:, 0:wd], A.t[:, 0:wd], AF.Ln, bias=K.epsb[1.0].t[:, 0:1], scale=1.0), reads=[A.b, K.epsb[1.0].b], writes=[Bq.b])
            S.op("act", lambda e: e.activation(A.t[:, 0:wd], Bq.t[:, 0:wd], AF.Exp, scale=-1.0), reads=[Bq.b], writes=[A.b])
            S.op("dve", lambda e: e.tensor_tensor(Bq.t[:, 0:wd], G, A.t[:, 0:wd], ALU.mult), reads=[K.pb[2 * p], A.b], writes=[Bq.b])
            S.op("dve", lambda e: e.tensor_tensor(h.t[:, j * 512:j * 512 + wd], Bq.t[:, 0:wd], U, ALU.mult), reads=[Bq.b, K.pb[2 * p + 1]], writes=[h.b])

        def stage_T(t):
            h = hh2[t % 2]
            pT = K.ps[:, 5 * 512:8 * 512].bitcast(BF16)
            for f in range(22):
                bi = 5 + (f * 128) // 1024
                S.op("pe", lambda e, f=f: e.transpose(pT[:, f * 128:(f + 1) * 128], h.t[:, f * 128:(f + 1) * 128], K.idb.t[:]),
                     reads=[h.b, K.idb.b], writes=[K.pb[bi]])
            hTf = hT.t[:].rearrange("p f t -> p (f t)")
            S.op("act", lambda e: e.copy(hTf[:, 0:1024], pT[:, 0:1024]), reads=[K.pb[5]], writes=[hT.b])
            S.op("dve", lambda e: e.tensor_copy(hTf[:, 1024:2048], pT[:, 1024:2048]), reads=[K.pb[6]], writes=[hT.b])
            S.op("act", lambda e: e.copy(hTf[:, 2048:2816], pT[:, 2048:2816]), reads=[K.pb[7]], writes=[hT.b])

        def stage_O(t):
            sl = t % 2
            x3 = xt[t % 3]
            for half in range(2):
                for f in range(22):
                    S.op("pe", lambda e, f=f, half=half: e.matmul(
                        bank(K, 5 + half), hT.t[:, f, :], w2.t[:, f, half * 512:(half + 1) * 512],
                        start=(f == 0), stop=(f == 21)),
                        reads=[hT.b, w2.bs[f]], writes=[K.pb[5 + half]])
            S.op("dve", lambda e: e.tensor_tensor(xo[sl].t[:], bank(K, 5, 2), x3.t[:], ALU.add),
                 reads=[K.pb[5], K.pb[6], x3.b], writes=[xo[sl].b])
            if final:
                S.op("act", lambda e: e.activation(junk2.t[:], xo[sl].t[:], AF.Square, accum_out=ss2.t[:, 0:1]),
                     reads=[xo[sl].b], writes=[junk2.b, ss2.b])
                rstd_ops(K, ss2.t[:, 0:1], ss2.t[:, 1:2], 1, 1.0 / D, NORM_EPS, [ss2.b], [ss2.b])
                S.op("dve", lambda e: e.scalar_tensor_tensor(xf[sl].t[:], xo[sl].t[:], ss2.t[:, 1:2], fwt.t[:], ALU.mult, ALU.mult),
                     reads=[xo[sl].b, ss2.b, fwt.b], writes=[xf[sl].b])
                S.dma("pool", dst[t * 128:(t + 1) * 128, :], xf[sl].t[:], reads=[xf[sl].b], writes=[K.xbuf[t]], ring="st")
            else:
                S.dma("pool", dst[t * 128:(t + 1) * 128, :], xo[sl].t[:], reads=[xo[sl].b], writes=[K.xbuf[t]], ring="st")

        prologue(0)
        for t in range(NT + 1):
            if t < NT:
                chunk(t, 0)
            if t >= 1:
                stage_T(t - 1)
            if t < NT:
                chunk(t, 1)
            if t >= 1:
                stage_O(t - 1)
            if t < NT:
                chunk(t, 2)
                if t + 1 < NT:
                    prologue(t + 1)
                chunk(t, 3)
                chunk(t, 4)
                chunk(t, 5)
        S.barrier()


def attn_a1(K, l, src):
    nc, S = K.nc, K.S
    j = l // 2
    SL, NSEQ = K.SL, K.NSEQ
    win = K.attn_w_in[j]
    with ExitStack() as es:
        wq = sbt(K, es, "wq", [128, 8, D], BF16, 8)
        wk = sbt(K, es, "wk", [128, 8, D], BF16, 8)
        wv = sbt(K, es, "wv", [128, 8, D], BF16, 8)
        wqr = sbt(K, es, "wqr", [128, 8, D], BF16)
        wkr = sbt(K, es, "wkr", [128, 8, D], BF16)
        gw = sbt(K, es, "gw", [128, D], F32)
        cosT = sbt(K, es, "cosT", [128, SL], F32)
        sinT = sbt(K, es, "sinT", [128, SL], F32)
        S.dma("sp", gw.t[:], K.norm_mix_w[l].partition_broadcast(128), writes=[gw.b])
        load_w(K, wq, win[:, 0:D], 8, 2)
        load_w(K, wk, win[:, D:2 * D], 8, 2)
        load_w(K, wv, win[:, 2 * D:3 * D], 8, 2)
        for (w, wr) in ((wq, wqr), (wk, wkr)):
            wv5 = w.t[:].rearrange("p k (g two d) -> p k g two d", two=2, d=32)
            wr5 = wr.t[:].rearrange("p k (g two d) -> p k g two d", two=2, d=32)
            for k in range(8):
                eng = "pool" if k % 2 == 0 else "dve"
                S.op(eng, lambda e, k=k, wr5=wr5, wv5=wv5: e.tensor_copy(wr5[:, k, :, 0, :], wv5[:, k, :, 1, :]), reads=[w.bs[k]], writes=[wr.b])
                S.op(eng, lambda e, k=k, wr5=wr5, wv5=wv5: e.tensor_copy(wr5[:, k, :, 1, :], wv5[:, k, :, 0, :]), reads=[w.bs[k]], writes=[wr.b])
        for s in range(NSEQ):
            with ExitStack() as es2:
                posi = sbt(K, es2, "posi", [128, SL], I32)
                u = sbt(K, es2, "u", [128, SL], F32)
                ni = sbt(K, es2, "ni", [128, SL], I32)
                S.dma("sp", posi.t[:], K.pos[s].partition_broadcast(128), writes=[posi.b])
                for (tab, ph) in ((sinT, 0.5), (cosT, 0.75)):
                    S.op("dve", lambda e, ph=ph: e.tensor_scalar(u.t[:], posi.t[:], K.cst.t[:, 0:1], ph, ALU.mult, ALU.add),
                         reads=[posi.b, K.cst.b], writes=[u.b])
                    S.op("dve", lambda e: e.tensor_copy(ni.t[:], u.t[:]), reads=[u.b], writes=[ni.b])
                    S.op("dve", lambda e: e.tensor_tensor(u.t[:], u.t[:], ni.t[:], ALU.subtract), reads=[u.b, ni.b], writes=[u.b])
                    S.op("dve", lambda e: e.tensor_scalar(u.t[:], u.t[:], -0.5, None, ALU.add), reads=[u.b], writes=[u.b])
                    S.op("dve", lambda e: e.scalar_tensor_tensor(u.t[:], u.t[:], -0.5, u.t[:], ALU.is_lt, ALU.add), reads=[u.b], writes=[u.b])
                    S.op("act", lambda e, tab=tab: e.activation(tab.t[:], u.t[:], AF.Sin, scale=2.0 * math.pi), reads=[u.b], writes=[tab.b])
                S.op("dve", lambda e: e.tensor_scalar(sinT.t[:], sinT.t[:], K.cst.t[:, 1:2], None, ALU.mult), reads=[sinT.b, K.cst.b], writes=[sinT.b])
                S.barrier()
            with ExitStack() as es3:
                xt = [sbt(K, es3, f"xt{i}", [128, D], F32) for i in range(2)]
                ss = [sbt(K, es3, f"ss{i}", [128, 2], F32) for i in range(2)]
                junk = sbt(K, es3, "junk", [128, D], BF16)
                xs = sbt(K, es3, "xs", [128, D], BF16)
                xT = sbt(K, es3, "xT", [128, 8, 512], BF16)
                t1 = [sbt(K, es3, f"t1{i}", [128, 512], F32) for i in range(2)]
                t2 = [sbt(K, es3, f"t2{i}", [128, 512], F32) for i in range(2)]
                ob = [sbt(K, es3, f"ob{i}", [128, 8, 512], BF16) for i in range(2)]
                vb = [sbt(K, es3, f"vb{i}", [128, 8, 129], BF16) for i in range(2)]
                for i in range(2):
                    S.op("pool", lambda e, i=i: e.memset(vb[i].t[:], 1.0), writes=[vb[i].b])
                cnt = 0
                vcnt = 0
                for tb in range(SL // 512):
                    for i in range(4):
                        sl = i % 2
                        r0 = s * SL + tb * 512 + i * 128
                        S.dma("sp", xt[sl].t[:], src[r0:r0 + 128, :], writes=[xt[sl].b])
                        norm_xs(K, xt[sl], gw, junk, ss[sl], xs)
                        transpose8(K, xs, 4, xT.t[:, :, i * 128:(i + 1) * 128], [xT.b], "act")
                    for qi, (w, wr, dram) in enumerate(((wq, wqr, K.QT), (wk, wkr, K.KT))):
                        o_ = ob[qi]
                        for hh in range(NH):
                            p = cnt % 2
                            cnt += 1
                            for (ww, pbi) in ((w, 2 * p), (wr, 2 * p + 1)):
                                for k in range(8):
                                    S.op("pe", lambda e, k=k, ww=ww, pbi=pbi: e.matmul(
                                        bank(K, pbi), ww.t[:, k, hh * 128:(hh + 1) * 128], xT.t[:, k, :],
                                        start=(k == 0), stop=(k == 7)),
                                        reads=[xT.b] + ww.bs, writes=[K.pb[pbi]])
                            cs = slice(tb * 512, (tb + 1) * 512)
                            S.op("dve", lambda e: e.tensor_tensor(t1[p].t[:], bank(K, 2 * p), cosT.t[:, cs], ALU.mult),
                                 reads=[K.pb[2 * p], cosT.b], writes=[t1[p].b])
                            S.op("dve", lambda e: e.tensor_tensor(t2[p].t[:], bank(K, 2 * p + 1), sinT.t[:, cs], ALU.mult),
                                 reads=[K.pb[2 * p + 1], sinT.b], writes=[t2[p].b])
                            S.op("pool", lambda e: e.tensor_tensor(o_.t[:, hh, :], t1[p].t[:], t2[p].t[:], ALU.add),
                                 reads=[t1[p].b, t2[p].b], writes=[o_.b])
                        S.dma("pool", dram[s].rearrange("h p s -> p h s")[:, :, tb * 512:(tb + 1) * 512], o_.t[:],
                              reads=[o_.b], writes=[K.scrb[(qi, s, tb)]], ring="st")
                    for i in range(4):
                        v_ = vb[vcnt % 2]
                        vcnt += 1
                        for half in range(2):
                            for k in range(8):
                                S.op("pe", lambda e, k=k, half=half: e.matmul(
                                    bank(K, 5 + half), xT.t[:, k, i * 128:(i + 1) * 128], wv.t[:, k, half * 512:(half + 1) * 512],
                                    start=(k == 0), stop=(k == 7)),
                                    reads=[xT.b] + wv.bs, writes=[K.pb[5 + half]])
                        S.op("act", lambda e: e.copy(v_.t[:, :, 0:128], bank(K, 5, 2).rearrange("p (h v) -> p h v", h=8)),
                             reads=[K.pb[5], K.pb[6]], writes=[v_.b])
                        r0 = s * SL + tb * 512 + i * 128
                        S.dma("pool", K.VA[r0:r0 + 128, :], v_.t[:].rearrange("p h v -> p (h v)"),
                              reads=[v_.b], writes=[K.scrb[(2, r0 // 128)]], ring="st")
                S.barrier()
        S.barrier()


def attn_a2(K, l, src, dst):
    nc, S = K.nc, K.S
    j = l // 2
    SL, NSEQ, T = K.SL, K.NSEQ, K.T
    lam_init = 0.8 - 0.6 * math.exp(-0.3 * l)
    with ExitStack() as es:
        wo = sbt(K, es, "wo", [128, 8, D], BF16, 8)
        KTs = sbt(K, es, "KTs", [128, 8, SL], BF16, 8)
        VAs = sbt(K, es, "VAs", [128, T, 1032], BF16, T)
        swb = sbt(K, es, "swb", [128, 128], F32)
        lv = sbt(K, es, "lv", [128, 4, 64], F32)
        lsum = sbt(K, es, "lsum", [128, 4], F32)
        lam8 = sbt(K, es, "lam8", [128, 4, 2], F32)
        QTb = [sbt(K, es, f"QTb{i}", [128, 8, 512], BF16) for i in range(2)]
        PT = [sbt(K, es, f"PT{i}", [128, 2, 512], BF16) for i in range(4)]
        xt = [sbt(K, es, f"xt{i}", [128, D], F32) for i in range(2)]
        on = sbt(K, es, "on", [128, 4, D], BF16, 4)
        onT = sbt(K, es, "onT", [128, 8, 128], BF16)
        r8 = sbt(K, es, "r8", [128, 4, 2], F32)
        od = sbt(K, es, "od", [128, 4, 128], F32)
        tm = sbt(K, es, "tm", [128, 4, 128], F32)
        sq4 = sbt(K, es, "sq4", [128, 8], F32)
        xo = [sbt(K, es, f"xo{i}", [128, D], F32) for i in range(2)]

        load_w(K, wo, K.attn_w_out[j], 8, 2)
        S.dma("sp", swb.t[:], K.attn_subln_w[j].partition_broadcast(128), writes=[swb.b])
        S.op("dve", lambda e: e.tensor_scalar(swb.t[:], swb.t[:], 1.0 - lam_init, None, ALU.mult), reads=[swb.b], writes=[swb.b])
        for i, a in enumerate((K.attn_lambda_q1, K.attn_lambda_k1, K.attn_lambda_q2, K.attn_lambda_k2)):
            S.dma("sp", lv.t[:, i, :], a[j].partition_broadcast(128), writes=[lv.b])
        S.op("dve", lambda e: e.tensor_tensor(lv.t[:, 0, :], lv.t[:, 0, :], lv.t[:, 1, :], ALU.mult), reads=[lv.b], writes=[lv.b])
        S.op("dve", lambda e: e.tensor_tensor(lv.t[:, 2, :], lv.t[:, 2, :], lv.t[:, 3, :], ALU.mult), reads=[lv.b], writes=[lv.b])
        S.op("dve", lambda e: e.tensor_reduce(lsum.t[:, 0:1], lv.t[:, 0, :], AX.X, ALU.add), reads=[lv.b], writes=[lsum.b])
        S.op("dve", lambda e: e.tensor_reduce(lsum.t[:, 1:2], lv.t[:, 2, :], AX.X, ALU.add), reads=[lv.b], writes=[lsum.b])
        S.op("act", lambda e: e.activation(lsum.t[:, 0:2], lsum.t[:, 0:2], AF.Exp), reads=[lsum.b], writes=[lsum.b])
        S.op("dve", lambda e: e.tensor_tensor(lsum.t[:, 2:3], lsum.t[:, 1:2], lsum.t[:, 0:1], ALU.subtract), reads=[lsum.b], writes=[lsum.b])
        S.op("dve", lambda e: e.tensor_scalar(lsum.t[:, 2:3], lsum.t[:, 2:3], -lam_init, None, ALU.add), reads=[lsum.b], writes=[lsum.b])
        S.op("dve", lambda e: e.memset(lam8.t[:], 1.0), writes=[lam8.b])
        for jj in range(4):
            S.op("dve", lambda e, jj=jj: e.tensor_copy(lam8.t[:, jj, 1:2], lsum.t[:, 2:3]), reads=[lsum.b], writes=[lam8.b])

        accv = K.ps[:, 2048:4096].rearrange("p (j c e) -> p j c e", j=4, c=2)
        accb = [K.pb[4 + jj] for jj in range(4)]
        pcnt = 0
        for s in range(NSEQ):
            for hh in range(NH):
                S.dma("sp", KTs.t[:, hh, :], K.KT[s, hh], reads=[K.scrb[(1, s, tb)] for tb in range(SL // 512)], writes=[KTs.bs[hh]])
            g4 = max(1, T // 4)
            for t0 in range(0, T, g4):
                S.dma("sp", VAs.t[:, t0:t0 + g4, :],
                      K.VA[s * SL + t0 * 128: s * SL + (t0 + g4) * 128, :].rearrange("(t p) c -> p t c", p=128),
                      reads=[K.scrb[(2, s * T + tt)] for tt in range(t0, t0 + g4)], writes=VAs.bs[t0:t0 + g4])
            units = [(qb, hh, kt) for qb in range(SL // 512) for hh in range(NH) for kt in range(4 * qb + 4)]
            ust = {}

            def emit_S(ui):
                nonlocal pcnt
                qb, hh, kt = units[ui]
                Q = QTb[qb % 2]
                if hh == 0 and kt == 0:
                    S.dma("sp", Q.t[:], K.QT[s].rearrange("h p s -> p h s")[:, :, qb * 512:(qb + 1) * 512], reads=[K.scrb[(0, s, qb)]], writes=[Q.b])
                di = kt - 4 * qb
                qlo = max(0, di) * 128
                p = pcnt % 2
                P_ = PT[pcnt % 4]
                pcnt += 1
                ust[ui] = (P_, di, qlo)
                for c in range(2):
                    S.op("pe", lambda e, c=c: e.matmul(
                        bank(K, 2 * p + c)[:, qlo:512], KTs.t[c * 64:(c + 1) * 64, hh, kt * 128:(kt + 1) * 128],
                        Q.t[c * 64:(c + 1) * 64, hh, qlo:512], start=True, stop=True),
                        reads=[KTs.bs[hh], Q.b], writes=[K.pb[2 * p + c]])
                sv = bank(K, 2 * p, 2).rearrange("p (c q) -> p c q", c=2)
                S.op("act", lambda e: e.activation(P_.t[:, :, qlo:512], sv[:, :, qlo:512], AF.Exp, scale=0.125),
                     reads=[K.pb[2 * p], K.pb[2 * p + 1]], writes=[P_.b])
                if di >= 0:
                    S.op("pool", lambda e: e.tensor_tensor(P_.t[:, :, qlo:qlo + 128], P_.t[:, :, qlo:qlo + 128], K.maskc2.t[:], ALU.mult),
                         reads=[P_.b, K.maskc2.b], writes=[P_.b])

            def emit_AV(ui):
                qb, hh, kt = units[ui]
                P_, di, qlo = ust.pop(ui)
                for c in range(2):
                    for jj in range(max(0, di), 4):
                        S.op("pe", lambda e, c=c, jj=jj: e.matmul(
                            accv[:, jj, c, 0:129], P_.t[:, c, jj * 128:(jj + 1) * 128],
                            VAs.t[:, kt, hh * 129:(hh + 1) * 129],
                            start=(kt == 0 and c == 0), stop=(kt == 4 * qb + jj), skip_group_check=True),
                            reads=[P_.b, VAs.bs[kt]], writes=[accb[jj]])

            def post_head(hh):
                S.op("dve", lambda e: e.reciprocal(r8.t[:], accv[:, :, :, 128]), reads=accb, writes=[r8.b])
                S.op("dve", lambda e: e.tensor_tensor(r8.t[:], r8.t[:], lam8.t[:], ALU.mult), reads=[r8.b, lam8.b], writes=[r8.b])
                S.op("dve", lambda e: e.tensor_tensor(od.t[:], accv[:, :, 0, 0:128], r8.t[:, :, 0:1].to_broadcast([128, 4, 128]), ALU.mult),
                     reads=accb + [r8.b], writes=[od.b])
                S.op("dve", lambda e: e.tensor_tensor(tm.t[:], accv[:, :, 1, 0:128], r8.t[:, :, 1:2].to_broadcast([128, 4, 128]), ALU.mult),
                     reads=accb + [r8.b], writes=[tm.b])
                S.op("pool", lambda e: e.tensor_tensor(od.t[:], od.t[:], tm.t[:], ALU.add), reads=[od.b, tm.b], writes=[od.b])
                S.op("pool", lambda e: e.tensor_tensor(tm.t[:], od.t[:], od.t[:], ALU.mult), reads=[od.b], writes=[tm.b])
                S.op("dve", lambda e: e.tensor_reduce(sq4.t[:, 0:4], tm.t[:], AX.X, ALU.add), reads=[tm.b], writes=[sq4.b])
                rstd_ops(K, sq4.t[:, 0:4], sq4.t[:, 4:8], 4, 1.0 / 128, SUBLN_EPS, [sq4.b], [sq4.b])
                S.op("dve", lambda e: e.tensor_tensor(tm.t[:], od.t[:], sq4.t[:, 4:8].unsqueeze(2).to_broadcast([128, 4, 128]), ALU.mult),
                     reads=[od.b, sq4.b], writes=[tm.b])
                S.op("pool", lambda e: e.tensor_tensor(on.t[:, :, hh * 128:(hh + 1) * 128], tm.t[:],
                                                       swb.t[:].unsqueeze(1).to_broadcast([128, 4, 128]), ALU.mult),
                     reads=[tm.b, swb.b], writes=on.bs)

            def out_proj(qb):
                nonlocal pcnt
                for jj in range(4):
                    r0 = s * SL + qb * 512 + jj * 128
                    S.dma("sp", xt[jj % 2].t[:], src[r0:r0 + 128, :], reads=[K.xbuf[r0 // 128]], writes=[xt[jj % 2].b])
                    p = pcnt % 2
                    pcnt += 1
                    pT = bank(K, 2 * p).bitcast(BF16)
                    for k in range(8):
                        S.op("pe", lambda e, k=k: e.transpose(pT[:, k * 128:(k + 1) * 128], on.t[:, jj, k * 128:(k + 1) * 128], K.idb.t[:]),
                             reads=[on.bs[jj], K.idb.b], writes=[K.pb[2 * p]])
                    S.op("dve", lambda e: e.tensor_copy(onT.t[:], pT[:, 0:1024].rearrange("p (k t) -> p k t", k=8)), reads=[K.pb[2 * p]], writes=[onT.b])
                    p = pcnt % 2
                    pcnt += 1
                    for half in range(2):
                        for k in range(8):
                            S.op("pe", lambda e, k=k, half=half: e.matmul(
                                bank(K, 2 * p + half), onT.t[:, k, :], wo.t[:, k, half * 512:(half + 1) * 512],
                                start=(k == 0), stop=(k == 7)),
                                reads=[onT.b] + wo.bs, writes=[K.pb[2 * p + half]])
                    x_ = xo[jj % 2]
                    S.op("dve", lambda e: e.tensor_tensor(x_.t[:], bank(K, 2 * p, 2), xt[jj % 2].t[:], ALU.add),
                         reads=[K.pb[2 * p], K.pb[2 * p + 1], xt[jj % 2].b], writes=[x_.b])
                    S.dma("pool", dst[r0:r0 + 128, :], x_.t[:], reads=[x_.b], writes=[K.xbuf[r0 // 128]], ring="st")

            emit_S(0)
            for ui, (qb, hh, kt) in enumerate(units):
                if ui + 1 < len(units):
                    emit_S(ui + 1)
                emit_AV(ui)
                if kt == 4 * qb + 3:
                    post_head(hh)
                    if hh == NH - 1:
                        out_proj(qb)
        S.barrier()


def hgrn_phase(K, l, src, dst):
    nc, S = K.nc, K.S
    j = l // 2
    SL, NSEQ, T = K.SL, K.NSEQ, K.T
    with ExitStack() as es:
        lbb = sbt(K, es, "lbb", [128, D], F32)
        with ExitStack() as es0:
            lp = sbt(K, es0, "lp", [128, 4, D], F32)
            den = sbt(K, es0, "den", [128, D], F32)
            for i in range(4):
                S.dma("sp", lp.t[:, i, :], K.hgrn_lb_param[i].partition_broadcast(128), writes=[lp.b])
            S.op("act", lambda e: e.activation(lp.t[:], lp.t[:], AF.Exp), reads=[lp.b], writes=[lp.b])
            S.op("dve", lambda e: e.tensor_tensor(den.t[:], lp.t[:, 0, :], lp.t[:, 1, :], ALU.add), reads=[lp.b], writes=[den.b])
            S.op("dve", lambda e: e.tensor_tensor(den.t[:], den.t[:], lp.t[:, 2, :], ALU.add), reads=[lp.b, den.b], writes=[den.b])
            S.op("dve", lambda e: e.tensor_tensor(den.t[:], den.t[:], lp.t[:, 3, :], ALU.add), reads=[lp.b, den.b], writes=[den.b])
            S.op("dve", lambda e: e.reciprocal(den.t[:], den.t[:]), reads=[den.b], writes=[den.b])
            S.op("dve", lambda e: e.memset(lbb.t[:], 0.0), writes=[lbb.b])
            for i in range(1, l + 1):
                S.op("dve", lambda e, i=i: e.tensor_tensor(lbb.t[:], lbb.t[:], lp.t[:, i, :], ALU.add), reads=[lp.b, lbb.b], writes=[lbb.b])
            S.op("dve", lambda e: e.tensor_tensor(lbb.t[:], lbb.t[:], den.t[:], ALU.mult), reads=[den.b, lbb.b], writes=[lbb.b])
            S.barrier()
        win = sbt(K, es, "win", [128, 8, 4 * D], BF16, 8)
        wout = sbt(K, es, "wout", [128, 8, D], BF16, 8)
        gw = sbt(K, es, "gw", [128, D], F32)
        gnb = sbt(K, es, "gnb", [128, 128], F32)
        xt = [sbt(K, es, f"xt{i}", [128, D], F32) for i in range(2)]
        ss = [sbt(K, es, f"ss{i}", [128, 2], F32) for i in range(2)]
        junk = sbt(K, es, "junk", [128, D], BF16)
        xs = sbt(K, es, "xs", [128, D], BF16)
        xT = sbt(K, es, "xT", [128, 8, 128], BF16)
        E = [sbt(K, es, f"E{i}", [128, D], F32) for i in range(4)]
        Ebg = [[sbt(K, es, f"Eb{i}{g}", [128, 512], F32) for g in range(2)] for i in range(2)]
        sq8g = [sbt(K, es, f"sq8g{g}", [128, 8], F32) for g in range(2)]
        logf = sbt(K, es, "logf", [128, D], F32)
        sgw2 = [sbt(K, es, f"sgw{i}", [128, D], F32) for i in range(2)]
        kd2 = [sbt(K, es, f"kd{i}", [128, D], BF16) for i in range(2)]
        qd2 = [sbt(K, es, f"qd{i}", [128, D], BF16) for i in range(2)]
        vv2 = [sbt(K, es, f"vv{i}", [128, D], BF16) for i in range(2)]
        scr2 = [sbt(K, es, f"scr{i}", [128, 8], F32) for i in range(2)]
        sb2 = [sbt(K, es, f"sb{i}", [128, 8], F32) for i in range(2)]
        onb = sbt(K, es, "onb", [128, D], BF16)
        qdTa = sbt(K, es, "qdTa", [128, 8, 128], BF16)
        qdTb = sbt(K, es, "qdTb", [128, 8, 128], BF16)
        kdT = sbt(K, es, "kdT", [128, 8, 128], BF16)
        Am = sbt(K, es, "Am", [128, 8, 128], BF16, 2)
        onT = sbt(K, es, "onT", [128, 8, 128], BF16)
        W = sbt(K, es, "W", [128, 8, 128], F32, 8)
        R0 = sbt(K, es, "R0", [128, 8, 128], BF16)
        R1 = sbt(K, es, "R1", [128, 8, 128], BF16, 2)
        dcur = sbt(K, es, "dcur", [128, 8, 3], F32)
        cprev = sbt(K, es, "cprev", [128, 8], F32)
        sq8 = sbt(K, es, "sq8", [128, 16], F32)
        xo = [sbt(K, es, f"xo{i}", [128, D], F32) for i in range(2)]

        S.dma("sp", gw.t[:], K.norm_mix_w[l].partition_broadcast(128), writes=[gw.b])
        S.dma("sp", gnb.t[:], K.hgrn_gnorm_w[j].partition_broadcast(128), writes=[gnb.b])
        load_w(K, win, K.hgrn_w_in[j], 8, 1)
        load_w(K, wout, K.hgrn_w_out[j], 8, 2)
        S.op("pool", lambda e: e.memset(qdTa.t[:], 0.0), writes=[qdTa.b])
        S.op("pool", lambda e: e.memset(qdTb.t[:], 0.0), writes=[qdTb.b])
        one = K.epsb[1.0]

        def proj(qi):
            p = next_pair(K)
            for half in range(2):
                for k in range(8):
                    S.op("pe", lambda e, k=k, half=half: e.matmul(
                        bank(K, 2 * p + half), xT.t[:, k, :], win.t[:, k, qi * D + half * 512: qi * D + (half + 1) * 512],
                        start=(k == 0), stop=(k == 7)),
                        reads=[xT.b, win.bs[k]], writes=[K.pb[2 * p + half]])
            return bank(K, 2 * p, 2), [K.pb[2 * p], K.pb[2 * p + 1]]

        def sig_chain(src_ps, src_b, A, Bq):
            S.op("act", lambda e: e.activation(A.t[:], src_ps, AF.Exp, scale=-1.0), reads=src_b, writes=[A.b])
            S.op("act", lambda e: e.activation(Bq.t[:], A.t[:], AF.Ln, bias=one.t[:, 0:1], scale=1.0), reads=[A.b, one.b], writes=[Bq.b])
            S.op("act", lambda e: e.activation(A.t[:], Bq.t[:], AF.Exp, scale=-1.0), reads=[Bq.b], writes=[A.b])

        def A1(t):
            sl = t % 2
            tl = t % T
            kd, scr_, sb_ = kd2[sl], scr2[sl], sb2[sl]
            if tl == 0:
                S.op("dve", lambda e: e.memset(cprev.t[:], 0.0), writes=[cprev.b])
            S.dma("sp", xt[sl].t[:], src[t * 128:(t + 1) * 128, :], reads=[K.xbuf[t]], writes=[xt[sl].b])
            norm_xs(K, xt[sl], gw, junk, ss[sl], xs)
            p = next_pair(K)
            transpose8(K, xs, 2 * p, xT.t[:], [xT.b], "act")
            fz, fzb = proj(1)
            S.op("act", lambda e: e.activation(E[0].t[:], fz, AF.Exp, scale=-1.0), reads=fzb, writes=[E[0].b])
            S.op("dve", lambda e: e.tensor_tensor(E[1].t[:], E[0].t[:], lbb.t[:], ALU.mult), reads=[E[0].b, lbb.b], writes=[E[1].b])
            S.op("act", lambda e: e.activation(E[1].t[:], E[1].t[:], AF.Ln, bias=one.t[:, 0:1], scale=1.0), reads=[E[1].b, one.b], writes=[E[1].b])
            S.op("act", lambda e: e.activation(E[0].t[:], E[0].t[:], AF.Ln, bias=one.t[:, 0:1], scale=1.0), reads=[E[0].b, one.b], writes=[E[0].b])
            S.op("dve", lambda e: e.tensor_tensor(logf.t[:], E[1].t[:], E[0].t[:], ALU.subtract), reads=[E[0].b, E[1].b], writes=[logf.b])
            p = next_pair(K)
            for half in range(2):
                S.op("pe", lambda e, half=half: e.matmul(bank(K, 2 * p + half), K.Tm.t[:], logf.t[:, half * 512:(half + 1) * 512], start=True, stop=True),
                     reads=[K.Tm.b, logf.b], writes=[K.pb[2 * p + half]])
            brel, brb = bank(K, 2 * p, 2), [K.pb[2 * p], K.pb[2 * p + 1]]
            pd = next_pair(K)
            dps = bank(K, 2 * pd)[:, 0:24].rearrange("p (h c) -> p h c", c=3)
            for hh in range(NH):
                S.op("pe", lambda e, hh=hh: e.matmul(dps[:, hh, :], logf.t[:, hh * 128:(hh + 1) * 128], K.Sel.t[:], start=True, stop=True),
                     reads=[logf.b, K.Sel.b], writes=[K.pb[2 * pd]])
            S.op("dve", lambda e: e.tensor_copy(dcur.t[:], dps), reads=[K.pb[2 * pd]], writes=[dcur.b])
            S.op("act", lambda e: e.activation(E[2].t[:], brel, AF.Exp), reads=brb, writes=[E[2].b])
            S.op("act", lambda e: e.activation(E[3].t[:], brel, AF.Exp, scale=-1.0), reads=brb, writes=[E[3].b])
            S.op("act", lambda e: e.activation(E[0].t[:], logf.t[:], AF.Exp), reads=[logf.b], writes=[E[0].b])
            S.op("dve", lambda e: e.tensor_scalar(E[0].t[:], E[0].t[:], -1.0, 1.0, ALU.mult, ALU.add), reads=[E[0].b], writes=[E[0].b])
            S.op("dve", lambda e: e.tensor_tensor(kd.t[:], E[0].t[:], E[3].t[:], ALU.mult), reads=[E[0].b, E[3].b], writes=[kd.b])
            S.op("dve", lambda e: e.tensor_tensor(scr_.t[:], cprev.t[:], dcur.t[:, :, 0], ALU.add), reads=[cprev.b, dcur.b], writes=[scr_.b])
            S.op("act", lambda e: e.activation(scr_.t[:], scr_.t[:], AF.Exp), reads=[scr_.b], writes=[scr_.b])
            S.op("act", lambda e: e.activation(sb_.t[:], dcur.t[:, :, 1], AF.Exp), reads=[dcur.b], writes=[sb_.b])
            S.op("dve", lambda e: e.tensor_copy(cprev.t[:], dcur.t[:, :, 2]), reads=[dcur.b], writes=[cprev.b])

        def A2(t):
            sl = t % 2
            qd = qd2[sl]
            q_, qb_ = proj(0)
            sig_chain(q_, qb_, E[0], E[1])
            S.op("dve", lambda e: e.tensor_tensor(E[1].t[:], q_, E[0].t[:], ALU.mult), reads=qb_ + [E[0].b], writes=[E[1].b])
            S.op("dve", lambda e: e.tensor_tensor(qd.t[:], E[1].t[:], E[2].t[:], ALU.mult), reads=[E[1].b, E[2].b], writes=[qd.b])

        def A3(t):
            sl = t % 2
            vv, sgw = vv2[sl], sgw2[sl]
            i_, ib_ = proj(2)
            S.op("act", lambda e: e.copy(vv.t[:], i_), reads=ib_, writes=[vv.b])
            g_, gb_ = proj(3)
            sig_chain(g_, gb_, E[0], E[1])
            S.op("dve", lambda e: e.tensor_tensor(E[1].t[:], g_, E[0].t[:], ALU.mult), reads=gb_ + [E[0].b], writes=[E[1].b])
            S.op("pool", lambda e: e.tensor_tensor(sgw.t[:].rearrange("p (h v) -> p h v", h=8), E[1].t[:].rearrange("p (h v) -> p h v", h=8),
                                                   gnb.t[:].unsqueeze(1).to_broadcast([128, 8, 128]), ALU.mult),
                 reads=[E[1].b, gnb.b], writes=[sgw.b])

        at = bank(K, 5).rearrange("p (h c) -> p h c", h=4)
        m0 = bank(K, 6).rearrange("p (h c) -> p h c", h=4)
        m1 = bank(K, 7).rearrange("p (h c) -> p h c", h=4)
        bst = {}

        def B0(t):
            sl = t % 2
            tl = t % T
            kd, qd, scr_ = kd2[sl], qd2[sl], scr2[sl]
            if tl == 0:
                S.op("dve", lambda e: e.memset(W.t[:], 0.0), writes=W.bs)
            S.op("pool", lambda e: e.tensor_tensor(R0.t[:], W.t[:], scr_.t[:].unsqueeze(2).to_broadcast([128, 8, 128]), ALU.mult),
                 reads=W.bs + [scr_.b], writes=[R0.b])
            p = next_pair(K)
            pT = bank(K, 2 * p).bitcast(BF16)
            for k in range(8):
                S.op("pe", lambda e, k=k: e.transpose(pT[:, k * 128:(k + 1) * 128], qd.t[:, k * 128:(k + 1) * 128], K.idb.t[:]),
                     reads=[qd.b, K.idb.b], writes=[K.pb[2 * p]])
            pv = pT[:, 0:1024].rearrange("p (k t) -> p k t", k=8)
            S.op("act", lambda e: e.copy(qdTa.t[:, :, 0:64], pv[:, :, 0:64]), reads=[K.pb[2 * p]], writes=[qdTa.b])
            S.op("dve", lambda e: e.tensor_copy(qdTb.t[:, :, 64:128], pv[:, :, 64:128]), reads=[K.pb[2 * p]], writes=[qdTb.b])
            p = next_pair(K)
            transpose8(K, kd, 2 * p, kdT.t[:], [kdT.b], "dve")

        def Bg_mm(t, hg):
            sl = t % 2
            kd, vv = kd2[sl], vv2[sl]
            hs = range(4 * hg, 4 * hg + 4)
            for hh in hs:
                S.op("pe", lambda e, hh=hh: e.matmul(at[:, hh % 4, 0:64], kdT.t[:, hh, :], qdTa.t[:, hh, 0:64], start=True, stop=True),
                     reads=[kdT.b, qdTa.b], writes=[K.pb[5]])
                S.op("pe", lambda e, hh=hh: e.matmul(at[:, hh % 4, 64:128], kdT.t[:, hh, :], qdTb.t[:, hh, 64:128], start=True, stop=True),
                     reads=[kdT.b, qdTb.b], writes=[K.pb[5]])
            for hh in hs:
                S.op("pe", lambda e, hh=hh: e.matmul(m0[:, hh % 4, :], kd.t[0:64, hh * 128:(hh + 1) * 128], vv.t[0:64, hh * 128:(hh + 1) * 128], start=True, stop=True),
                     reads=[kd.b, vv.b], writes=[K.pb[6]])
            for hh in hs:
                S.op("pe", lambda e, hh=hh: e.matmul(m1[:, hh % 4, :], kd.t[64:128, hh * 128:(hh + 1) * 128], vv.t[64:128, hh * 128:(hh + 1) * 128], start=True, stop=True),
                     reads=[kd.b, vv.b], writes=[K.pb[7]])
            S.op("dve", lambda e: e.tensor_tensor(Am.t[:, 4 * hg:4 * hg + 4, :], at, K.mask64.t[:].unsqueeze(1).to_broadcast([128, 4, 128]), ALU.mult),
                 reads=[K.pb[5], K.mask64.b], writes=[Am.bs[hg]])

        def Bg_state(t, hg):
            sl = t % 2
            scr_, sb_, vv = scr2[sl], sb2[sl], vv2[sl]
            o4 = bank(K, 4).rearrange("p (h v) -> p h v", h=4)
            hs = range(4 * hg, 4 * hg + 4)
            for hh in hs:
                S.op("dve", lambda e, hh=hh: e.scalar_tensor_tensor(W.t[:, hh, :], W.t[:, hh, :], scr_.t[:, hh:hh + 1], m0[:, hh % 4, :], ALU.mult, ALU.add),
                     reads=[W.bs[hh], scr_.b, K.pb[6], R0.b], writes=[W.bs[hh]])
            S.op("pool", lambda e: e.tensor_tensor(R1.t[:, 4 * hg:4 * hg + 4, :], W.t[:, 4 * hg:4 * hg + 4, :],
                                                   sb_.t[:, 4 * hg:4 * hg + 4].unsqueeze(2).to_broadcast([128, 4, 128]), ALU.mult),
                 reads=W.bs[4 * hg:4 * hg + 4] + [sb_.b], writes=[R1.bs[hg]])
            for hh in hs:
                oh = o4[:, hh % 4, :]
                pbo = K.pb[4]
                S.op("pe", lambda e, hh=hh, oh=oh: e.matmul(oh, Am.t[:, hh, :], vv.t[:, hh * 128:(hh + 1) * 128], start=True, stop=False),
                     reads=[Am.bs[hg], vv.b], writes=[pbo])
                S.op("pe", lambda e, hh=hh, oh=oh: e.matmul(oh, qdTa.t[:, hh, :], R0.t[:, hh, :], start=False, stop=False),
                     reads=[qdTa.b, R0.b], writes=[pbo])
                S.op("pe", lambda e, hh=hh, oh=oh: e.matmul(oh, qdTb.t[:, hh, :], R1.t[:, hh, :], start=False, stop=True),
                     reads=[qdTb.b, R1.bs[hg]], writes=[pbo])
            for hh in hs:
                S.op("dve", lambda e, hh=hh: e.scalar_tensor_tensor(W.t[:, hh, :], W.t[:, hh, :], sb_.t[:, hh:hh + 1], m1[:, hh % 4, :], ALU.mult, ALU.add),
                     reads=[W.bs[hh], sb_.b, K.pb[7], R1.bs[hg]], writes=[W.bs[hh]])
            sgw = sgw2[sl]
            cs = slice(512 * hg, 512 * hg + 512)
            e0, e1, sq = Ebg[0][hg], Ebg[1][hg], sq8g[hg]
            S.op("act", lambda e: e.activation(e0.t[:], bank(K, 4), AF.Square), reads=[K.pb[4]], writes=[e0.b])
            S.op("dve", lambda e: e.tensor_reduce(sq.t[:, 0:4], e0.t[:].rearrange("p (h v) -> p h v", h=4), AX.X, ALU.add), reads=[e0.b], writes=[sq.b])
            rstd_ops(K, sq.t[:, 0:4], sq.t[:, 4:8], 4, 1.0 / 128, GN_EPS, [sq.b], [sq.b])
            S.op("dve", lambda e: e.tensor_tensor(e1.t[:].rearrange("p (h v) -> p h v", h=4), o4,
                                                  sq.t[:, 4:8].unsqueeze(2).to_broadcast([128, 4, 128]), ALU.mult),
                 reads=[K.pb[4], sq.b], writes=[e1.b])
            S.op("pool", lambda e: e.tensor_tensor(onb.t[:, cs], e1.t[:], sgw.t[:, cs], ALU.mult), reads=[e1.b, sgw.b], writes=[onb.b])

        def B3(t):
            sl = t % 2
            p = next_pair(K)
            transpose8(K, onb, 2 * p, onT.t[:], [onT.b], "act")
            p = next_pair(K)
            for half in range(2):
                for k in range(8):
                    S.op("pe", lambda e, k=k, half=half: e.matmul(
                        bank(K, 2 * p + half), onT.t[:, k, :], wout.t[:, k, half * 512:(half + 1) * 512],
                        start=(k == 0), stop=(k == 7)),
                        reads=[onT.b] + wout.bs, writes=[K.pb[2 * p + half]])
            S.op("dve", lambda e: e.tensor_tensor(xo[sl].t[:], bank(K, 2 * p, 2), xt[sl].t[:], ALU.add),
                 reads=[K.pb[2 * p], K.pb[2 * p + 1], xt[sl].b], writes=[xo[sl].b])
            S.dma("pool", dst[t * 128:(t + 1) * 128, :], xo[sl].t[:], reads=[xo[sl].b], writes=[K.xbuf[t]], ring="st")

        NTT = NSEQ * T
        A1(0); A2(0); A3(0)
        for t in range(NTT):
            nx = t + 1 < NTT
            B0(t)
            Bg_mm(t, 0)
            if nx:
                A1(t + 1)
            Bg_state(t, 0)
            Bg_mm(t, 1)
            if nx:
                A2(t + 1)
            Bg_state(t, 1)
            B3(t)
            if nx:
                A3(t + 1)
        S.barrier()


def host_consts():
    ident = np.eye(128, dtype=np.float32)
    kk = np.arange(128)[:, None]
    qq = np.arange(128)[None, :]
    maskc = (kk <= qq).astype(np.float32)
    mask64 = ((kk <= qq) & ((kk // 64) == (qq // 64))).astype(np.float32)
    Tm = np.zeros((128, 128), np.float32)
    for c in range(128):
        base = (c // 64) * 64
        ref = base + 31
        if c > ref:
            Tm[ref + 1:c + 1, c] = 1.0
        elif c < ref:
            Tm[c + 1:ref + 1, c] = -1.0
    Sel = np.zeros((128, 3), np.float32)
    Sel[0:32, 0] = 1.0
    Sel[32:96, 1] = 1.0
    Sel[96:128, 2] = 1.0
    p = np.arange(128)
    invf = 1.0 / (10000.0 ** ((p % 32).astype(np.float64) / 32.0))
    cst = np.zeros((128, 4), np.float32)
    cst[:, 0] = (invf / (2.0 * np.pi)).astype(np.float32)
    cst[:, 1] = np.where((p % 64) < 32, -1.0, 1.0)
    cst[:, 2] = 1.0
    return dict(c_ident=ident, c_maskc=maskc, c_mask64=mask64, c_Tm=Tm, c_Sel=Sel, c_cst=cst)


WSPECS = [
    ("norm_mix_w", [4, D]), ("norm_ffn_w", [4, D]), ("final_norm_w", [D]),
    ("attn_w_in", [2, D, 3 * D]), ("attn_w_out", [2, D, D]),
    ("attn_lambda_q1", [2, 64]), ("attn_lambda_k1", [2, 64]), ("attn_lambda_q2", [2, 64]), ("attn_lambda_k2", [2, 64]),
    ("attn_subln_w", [2, 128]), ("hgrn_w_in", [2, D, 4 * D]), ("hgrn_w_out", [2, D, D]),
    ("hgrn_gnorm_w", [2, 128]), ("hgrn_lb_param", [4, D]),
    ("ffn_w_in", [4, D, 2 * FH]), ("ffn_w_out", [4, FH, D]),
]
CSPECS = [("c_ident", [128, 128]), ("c_maskc", [128, 128]), ("c_mask64", [128, 128]), ("c_Tm", [128, 128]),
          ("c_Sel", [128, 3]), ("c_cst", [128, 4])]


def build(NSEQ=2, SL=4096, prog=None):
    if prog is None:
        prog = []
        for l in range(4):
            prog.append(("attn" if l % 2 == 0 else "hgrn", l))
            prog.append(("ffn", l))
    nc = bass.Bass("TRN2", target_bir_lowering=False)
    K = Ctx()
    K.nc = nc
    K.uid = 0
    K.NSEQ, K.SL = NSEQ, SL
    K.T = SL // 128
    K.NT = NSEQ * K.T
    NTOK = NSEQ * SL
    x_in = nc.dram_tensor("x", [NTOK, D], F32, kind="ExternalInput").ap()
    K.pos = nc.dram_tensor("positions", [NSEQ, SL], I32, kind="ExternalInput").ap()
    for name, shp in WSPECS:
        setattr(K, name, nc.dram_tensor(name, shp, F32, kind="ExternalInput").ap())
    cd = {name: nc.dram_tensor(name, shp, F32, kind="ExternalInput").ap() for name, shp in CSPECS}
    out = nc.dram_tensor("out", [NTOK, D], F32, kind="ExternalOutput").ap()
    xres = nc.dram_tensor("xres", [NTOK, D], F32).ap()
    K.QT = nc.dram_tensor("QT", [NSEQ, NH, 128, SL], BF16).ap()
    K.KT = nc.dram_tensor("KT", [NSEQ, NH, 128, SL], BF16).ap()
    K.VA = nc.dram_tensor("VA", [NTOK, 1032], BF16).ap()
    with ExitStack() as es:
        S = Sched(nc, es)
        K.S = S
        K.ps = es.enter_context(nc.psum_tensor("ps", [128, 4096], F32))
        K.pb = [Buf(f"pb{i}") for i in range(8)]
        K.pair_i = 0
        K.xbuf = [Buf() for _ in range(K.NT)]
        K.scrb = {}
        for s_ in range(NSEQ):
            for tb_ in range(SL // 512):
                K.scrb[(0, s_, tb_)] = Buf()
                K.scrb[(1, s_, tb_)] = Buf()
        for t_ in range(K.NT):
            K.scrb[(2, t_)] = Buf()
        K.idb = sbt(K, es, "idb", [128, 128], BF16)
        K.maskc2 = sbt(K, es, "maskc2", [128, 2, 128], BF16)
        K.mask64 = sbt(K, es, "mask64", [128, 128], F32)
        K.Tm = sbt(K, es, "Tm", [128, 128], F32)
        K.Sel = sbt(K, es, "Sel", [128, 3], F32)
        K.cst = sbt(K, es, "cst", [128, 4], F32)
        K.epsb = {}
        for v in (NORM_EPS, SUBLN_EPS, 1.0):
            if v not in K.epsb:
                tl_ = sbt(K, es, f"eps{len(K.epsb)}", [128, 1], F32)
                S.op("dve", lambda e, tl_=tl_, v=v: e.memset(tl_.t[:], v), writes=[tl_.b])
                K.epsb[v] = tl_
        with ExitStack() as es0:
            tmpf = sbt(K, es0, "tmpf", [128, 128], F32)
            S.dma("sp", tmpf.t[:], cd["c_ident"], writes=[tmpf.b])
            S.op("dve", lambda e: e.tensor_copy(K.idb.t[:], tmpf.t[:]), reads=[tmpf.b], writes=[K.idb.b])
            tmpm = sbt(K, es0, "tmpm", [128, 128], F32)
            S.dma("sp", tmpm.t[:], cd["c_maskc"], writes=[tmpm.b])
            for c in range(2):
                S.op("dve", lambda e, c=c: e.tensor_copy(K.maskc2.t[:, c, :], tmpm.t[:]), reads=[tmpm.b], writes=[K.maskc2.b])
            S.dma("sp", K.mask64.t[:], cd["c_mask64"], writes=[K.mask64.b])
            S.dma("sp", K.Tm.t[:], cd["c_Tm"], writes=[K.Tm.b])
            S.dma("sp", K.Sel.t[:], cd["c_Sel"], writes=[K.Sel.b])
            S.dma("sp", K.cst.t[:], cd["c_cst"], writes=[K.cst.b])
            S.barrier()
        cur = x_in
        nph = len(prog)
        for pi, (kind, l) in enumerate(prog):
            last = pi == nph - 1
            dst = out if last else xres
            if kind == "ffn":
                ffn_phase(K, l, cur, dst, final=last)
            elif kind == "attn":
                attn_a1(K, l, cur)
                attn_a2(K, l, cur, dst)
            elif kind == "hgrn":
                hgrn_phase(K, l, cur, dst)
            cur = dst
        S.barrier()
        K.stats = (S.nsem, dict(S.n_inst), dict(S.n_wait))
    return nc, K


_CACHE = {}


def kernel(**inputs):
    x = np.asarray(inputs["x"], np.float32)
    pos = np.asarray(inputs["positions"], np.int32)
    B, SLn, _ = x.shape
    ncore = 8
    nseq = B // ncore
    key = (nseq, SLn)
    if key not in _CACHE:
        _CACHE[key] = build(nseq, SLn)[0]
    nc = _CACHE[key]
    consts = host_consts()
    wts = {name: np.ascontiguousarray(np.asarray(inputs[name], np.float32)) for name, _ in WSPECS}
    in_maps = []
    for c in range(ncore):
        m = {"x": np.ascontiguousarray(x[c * nseq:(c + 1) * nseq].reshape(nseq * SLn, D)),
             "positions": np.ascontiguousarray(pos[c * nseq:(c + 1) * nseq])}
        m.update(wts)
        m.update(consts)
        in_maps.append(m)
    res = run_bass_kernel_spmd(nc, in_maps, core_ids=list(range(ncore)))
    outs = [np.asarray(r["out"], np.float32).reshape(nseq, SLn, D) for r in res.results]
    return np.concatenate(outs, axis=0)
```

```python
import math
from contextlib import ExitStack

import numpy as np
import concourse.bass as bass
import concourse.mybir as mybir
from concourse.bass_utils import run_bass_kernel_spmd

F32 = mybir.dt.float32
BF16 = mybir.dt.bfloat16
I32 = mybir.dt.int32
AF = mybir.ActivationFunctionType
ALU = mybir.AluOpType
AX = mybir.AxisListType

D = 1024
FH = 2816
NH = 8
SEM_LIMIT = 30000
NORM_EPS = 1e-6
SUBLN_EPS = 1e-5
GN_EPS = 1e-6


class Counter:
    def __init__(self, S):
        self.S = S
        self.sem = None
        self.val = 0

    def peek(self):
        return (self.sem, self.val)

    def next(self, inc):
        if self.sem is None or self.val + inc > SEM_LIMIT:
            self.sem = self.S.new_sem()
            self.val = 0
        self.val += inc
        return (self.sem, self.val)


class Buf:
    __slots__ = ("name", "w", "r")

    def __init__(self, name=""):
        self.name = name
        self.w = []
        self.r = []


class Sched:
    def __init__(self, nc, es, same_engine_sync=("act", "dve", "pool")):
        self.nc = nc
        self.es = es
        self.eng = {"pe": nc.tensor, "act": nc.scalar, "dve": nc.vector,
                    "pool": nc.gpsimd, "sp": nc.sync}
        self.ctr = {e: Counter(self) for e in self.eng}
        self.waited = {e: {} for e in self.eng}
        self.same = set(same_engine_sync)
        self.nsem = 0
        self.rings = {}
        self.n_inst = {e: 0 for e in self.eng}
        self.n_wait = {e: 0 for e in self.eng}

    def new_sem(self):
        self.nsem += 1
        return self.es.enter_context(self.nc.semaphore(f"s{self.nsem}"))

    def _wait(self, E, tok):
        sem, val = tok
        if sem is None or val <= 0:
            return
        w = self.waited[E]
        k = id(sem)
        if w.get(k, 0) >= val:
            return
        w[k] = val
        self.eng[E].wait_ge(sem, val)
        self.n_wait[E] += 1

    def _deps(self, E, reads, writes):
        own = self.ctr[E].sem
        for b in reads:
            for t in b.w:
                if t[0] is own and E not in self.same:
                    continue
                self._wait(E, t)
        for b in writes:
            for t in b.w + b.r:
                if t[0] is own and E not in self.same:
                    continue
                self._wait(E, t)

    def _commit(self, tok, reads, writes):
        for b in reads:
            b.r.append(tok)
            if len(b.r) > 32:
                b.r = b.r[-32:]
        for b in writes:
            b.w = [tok]
            b.r = []

    def op(self, E, fn, reads=(), writes=()):
        self._deps(E, reads, writes)
        inst = fn(self.eng[E])
        tok = self.ctr[E].next(1)
        inst.then_inc(tok[0], 1)
        self.n_inst[E] += 1
        self._commit(tok, reads, writes)
        return tok

    def dma(self, Q, out, in_, reads=(), writes=(), ring="ld", nring=8):
        key = (Q, ring)
        if key not in self.rings:
            self.rings[key] = [[Counter(self) for _ in range(nring)], 0]
        rg = self.rings[key]
        c = rg[0][rg[1] % len(rg[0])]
        rg[1] += 1
        self._wait(Q, c.peek())
        self._deps(Q, reads, writes)
        tok = c.next(16)
        self.eng[Q].dma_start(out=out, in_=in_).then_inc(tok[0], 16)
        self.n_inst[Q] += 1
        self._commit(tok, reads, writes)
        return tok

    def barrier(self):
        toks = [c.peek() for c in self.ctr.values()]
        for rg in self.rings.values():
            toks += [c.peek() for c in rg[0]]
        for E in self.eng:
            for t in toks:
                self._wait(E, t)


class Tl:
    def __init__(self, t, n=1):
        self.t = t
        self.bs = [Buf() for _ in range(n)]

    @property
    def b(self):
        return self.bs[0]


class Ctx:
    pass


def sbt(K, es, name, shape, dt, n=1):
    K.uid += 1
    return Tl(es.enter_context(K.nc.sbuf_tensor(f"{name}_{K.uid}", shape, dt)), n)


def bank(K, i, n=1):
    return K.ps[:, i * 512:(i + n) * 512]


def next_pair(K):
    p = K.pair_i % 2
    K.pair_i += 1
    return p


def rstd_ops(K, ss, rs, n, inv_n, eps, rd, wr):
    S = K.S
    S.op("act", lambda e: e.activation(rs, ss, AF.Ln, bias=K.epsb[eps].t[:, 0:1], scale=inv_n), reads=rd + [K.epsb[eps].b], writes=wr)
    S.op("act", lambda e: e.activation(rs, rs, AF.Exp, scale=-0.5), reads=wr, writes=wr)


def norm_xs(K, xt, gw, junk, ss, xs):
    S = K.S
    S.op("act", lambda e: e.activation(junk.t[:], xt.t[:], AF.Square, accum_out=ss.t[:, 0:1]),
         reads=[xt.b], writes=[junk.b, ss.b])
    rstd_ops(K, ss.t[:, 0:1], ss.t[:, 1:2], 1, 1.0 / D, NORM_EPS, [ss.b], [ss.b])
    S.op("dve", lambda e: e.scalar_tensor_tensor(xs.t[:], xt.t[:], ss.t[:, 1:2], gw.t[:], ALU.mult, ALU.mult),
         reads=[xt.b, ss.b, gw.b], writes=[xs.b])


def transpose8(K, src, pbank_i, dst_ap, dst_bufs, eng="act", nblk=8):
    S = K.S
    pT = bank(K, pbank_i).bitcast(BF16)
    for k in range(nblk):
        S.op("pe", lambda e, k=k: e.transpose(pT[:, k * 128:(k + 1) * 128], src.t[:, k * 128:(k + 1) * 128], K.idb.t[:]),
             reads=[src.b, K.idb.b], writes=[K.pb[pbank_i]])
    pv = pT[:, 0:nblk * 128].rearrange("p (k t) -> p k t", k=nblk)
    if eng == "act":
        S.op("act", lambda e: e.copy(dst_ap, pv), reads=[K.pb[pbank_i]], writes=dst_bufs)
    else:
        S.op(eng, lambda e: e.tensor_copy(dst_ap, pv), reads=[K.pb[pbank_i]], writes=dst_bufs)


def load_w(K, wt, src, nk, per=1):
    v = src.rearrange("(k p) n -> p k n", p=128)
    for k0 in range(0, nk, per):
        k1 = min(nk, k0 + per)
        K.S.dma("pool", wt.t[:, k0:k1, :], v[:, k0:k1, :], writes=wt.bs[k0:k1], ring="w")


def silu_sig(K, src_ps, src_buf, ea, eb):
    S = K.S
    S.op("act", lambda e: e.activation(ea.t[:], src_ps, AF.Exp, scale=-1.0), reads=[src_buf], writes=[ea.b])
    S.op("act", lambda e: e.activation(eb.t[:], ea.t[:], AF.Ln, bias=K.epsb[1.0].t[:, 0:1], scale=1.0), reads=[ea.b, K.epsb[1.0].b], writes=[eb.b])
    S.op("act", lambda e: e.activation(ea.t[:], eb.t[:], AF.Exp, scale=-1.0), reads=[eb.b], writes=[ea.b])


def ffn_phase(K, l, src, dst, final):
    nc, S = K.nc, K.S
    NT = K.NT
    with ExitStack() as es:
        w1 = sbt(K, es, "w1", [128, 8, 2 * FH], BF16, 12)
        w2 = sbt(K, es, "w2", [128, 22, D], BF16, 22)
        gw = sbt(K, es, "gw", [128, D], F32)
        fwt = sbt(K, es, "fwt", [128, D], F32) if final else None
        xt = [sbt(K, es, f"xt{i}", [128, D], F32) for i in range(3)]
        ss = [sbt(K, es, f"ss{i}", [128, 2], F32) for i in range(2)]
        junk = sbt(K, es, "junk", [128, D], BF16)
        junk2 = junk
        xs = sbt(K, es, "xs", [128, D], BF16)
        xT = [sbt(K, es, f"xT{i}", [128, 8, 128], BF16) for i in range(2)]
        ea = [sbt(K, es, f"ea{i}", [128, 512], F32) for i in range(2)]
        eb = [sbt(K, es, f"eb{i}", [128, 512], F32) for i in range(2)]
        hh2 = [sbt(K, es, f"h{i}", [128, FH], BF16) for i in range(2)]
        hT = sbt(K, es, "hT", [128, 22, 128], BF16)
        xo = [sbt(K, es, f"xo{i}", [128, D], F32) for i in range(2)]
        xf = [sbt(K, es, f"xf{i}", [128, D], F32) for i in range(2)] if final else None
        ss2 = sbt(K, es, "ss2", [128, 2], F32)

        S.dma("sp", gw.t[:], K.norm_ffn_w[l].partition_broadcast(128), writes=[gw.b])
        if final:
            S.dma("sp", fwt.t[:], K.final_norm_w.partition_broadcast(128), writes=[fwt.b])
        w1v = K.ffn_w_in[l].rearrange("(k p) n -> p k n", p=128)
        for j in range(6):
            wd = 512 if j < 5 else 256
            for u in range(2):
                c0 = u * FH + j * 512
                S.dma("pool", w1.t[:, :, c0:c0 + wd], w1v[:, :, c0:c0 + wd], writes=[w1.bs[6 * u + j]], ring="w")
        load_w(K, w2, K.ffn_w_out[l], 22, 4)

        def prologue(t):
            sl = t % 2
            x3 = xt[t % 3]
            S.dma("sp", x3.t[:], src[t * 128:(t + 1) * 128, :], writes=[x3.b])
            norm_xs(K, x3, gw, junk, ss[sl], xs)
            transpose8(K, xs, 4, xT[sl].t[:], [xT[sl].b], "act")

        def chunk(t, j):
            sl = t % 2
            h = hh2[t % 2]
            wd = 512 if j < 5 else 256
            p = j % 2
            G, U = bank(K, 2 * p)[:, 0:wd], bank(K, 2 * p + 1)[:, 0:wd]
            for which, pb_i, off, u in ((G, 2 * p, 0, 0), (U, 2 * p + 1, FH, 1)):
                for k in range(8):
                    S.op("pe", lambda e, k=k, which=which, off=off: e.matmul(
                        which, xT[sl].t[:, k, :], w1.t[:, k, off + j * 512: off + j * 512 + wd],
                        start=(k == 0), stop=(k == 7)),
                        reads=[xT[sl].b, w1.bs[6 * u + j]], writes=[K.pb[pb_i]])
            A, Bq = ea[p], eb[p]
            S.op("act", lambda e: e.activation(A.t[:, 0:wd], G, AF.Exp, scale=-1.0), reads=[K.pb[2 * p]], writes=[A.b])
            S.op("act", lambda e: e.activation(Bq.t[:, 0:wd], A.t[:, 0:wd], AF.Ln, bias=K.epsb[1.0].t[:, 0:1], scale=1.0), reads=[A.b, K.epsb[1.0].b], writes=[Bq.b])
            S.op("act", lambda e: e.activation(A.t[:, 0:wd], Bq.t[:, 0:wd], AF.Exp, scale=-1.0), reads=[Bq.b], writes=[A.b])
            S.op("dve", lambda e: e.tensor_tensor(Bq.t[:, 0:wd], G, A.t[:, 0:wd], ALU.mult), reads=[K.pb[2 * p], A.b], writes=[Bq.b])
            S.op("dve", lambda e: e.tensor_tensor(h.t[:, j * 512:j * 512 + wd], Bq.t[:, 0:wd], U, ALU.mult), reads=[Bq.b, K.pb[2 * p + 1]], writes=[h.b])

        def stage_T(t):
            h = hh2[t % 2]
            pT = K.ps[:, 5 * 512:8 * 512].bitcast(BF16)
            for f in range(22):
                bi = 5 + (f * 128) // 1024
                S.op("pe", lambda e, f=f: e.transpose(pT[:, f * 128:(f + 1) * 128], h.t[:, f * 128:(f + 1) * 128], K.idb.t[:]),
                     reads=[h.b, K.idb.b], writes=[K.pb[bi]])
            hTf = hT.t[:].rearrange("p f t -> p (f t)")
            S.op("act", lambda e: e.copy(hTf[:, 0:1024], pT[:, 0:1024]), reads=[K.pb[5]], writes=[hT.b])
            S.op("dve", lambda e: e.tensor_copy(hTf[:, 1024:2048], pT[:, 1024:2048]), reads=[K.pb[6]], writes=[hT.b])
            S.op("act", lambda e: e.copy(hTf[:, 2048:2816], pT[:, 2048:2816]), reads=[K.pb[7]], writes=[hT.b])

        def stage_O(t):
            sl = t % 2
            x3 = xt[t % 3]
            for half in range(2):
                for f in range(22):
                    S.op("pe", lambda e, f=f, half=half: e.matmul(
                        bank(K, 5 + half), hT.t[:, f, :], w2.t[:, f, half * 512:(half + 1) * 512],
                        start=(f == 0), stop=(f == 21)),
                        reads=[hT.b, w2.bs[f]], writes=[K.pb[5 + half]])
            S.op("dve", lambda e: e.tensor_tensor(xo[sl].t[:], bank(K, 5, 2), x3.t[:], ALU.add),
                 reads=[K.pb[5], K.pb[6], x3.b], writes=[xo[sl].b])
            if final:
                S.op("act", lambda e: e.activation(junk2.t[:], xo[sl].t[:], AF.Square, accum_out=ss2.t[:, 0:1]),
                     reads=[xo[sl].b], writes=[junk2.b, ss2.b])
                rstd_ops(K, ss2.t[:, 0:1], ss2.t[:, 1:2], 1, 1.0 / D, NORM_EPS, [ss2.b], [ss2.b])
                S.op("dve", lambda e: e.scalar_tensor_tensor(xf[sl].t[:], xo[sl].t[:], ss2.t[:, 1:2], fwt.t[:], ALU.mult, ALU.mult),
                     reads=[xo[sl].b, ss2.b, fwt.b], writes=[xf[sl].b])
                S.dma("pool", dst[t * 128:(t + 1) * 128, :], xf[sl].t[:], reads=[xf[sl].b], writes=[K.xbuf[t]], ring="st")
            else:
                S.dma("pool", dst[t * 128:(t + 1) * 128, :], xo[sl].t[:], reads=[xo[sl].b], writes=[K.xbuf[t]], ring="st")

        prologue(0)
        for t in range(NT + 1):
            if t < NT:
                chunk(t, 0)
            if t >= 1:
                stage_T(t - 1)
            if t < NT:
                chunk(t, 1)
            if t >= 1:
                stage_O(t - 1)
            if t < NT:
                chunk(t, 2)
                if t + 1 < NT:
                    prologue(t + 1)
                chunk(t, 3)
                chunk(t, 4)
                chunk(t, 5)
        S.barrier()


def attn_a1(K, l, src):
    nc, S = K.nc, K.S
    j = l // 2
    SL, NSEQ = K.SL, K.NSEQ
    win = K.attn_w_in[j]
    with ExitStack() as es:
        wq = sbt(K, es, "wq", [128, 8, D], BF16, 8)
        wk = sbt(K, es, "wk", [128, 8, D], BF16, 8)
        wv = sbt(K, es, "wv", [128, 8, D], BF16, 8)
        wqr = sbt(K, es, "wqr", [128, 8, D], BF16)
        wkr = sbt(K, es, "wkr", [128, 8, D], BF16)
        gw = sbt(K, es, "gw", [128, D], F32)
        cosT = sbt(K, es, "cosT", [128, SL], F32)
        sinT = sbt(K, es, "sinT", [128, SL], F32)
        S.dma("sp", gw.t[:], K.norm_mix_w[l].partition_broadcast(128), writes=[gw.b])
        load_w(K, wq, win[:, 0:D], 8, 2)
        load_w(K, wk, win[:, D:2 * D], 8, 2)
        load_w(K, wv, win[:, 2 * D:3 * D], 8, 2)
        for (w, wr) in ((wq, wqr), (wk, wkr)):
            wv5 = w.t[:].rearrange("p k (g two d) -> p k g two d", two=2, d=32)
            wr5 = wr.t[:].rearrange("p k (g two d) -> p k g two d", two=2, d=32)
            for k in range(8):
                eng = "pool" if k % 2 == 0 else "dve"
                S.op(eng, lambda e, k=k, wr5=wr5, wv5=wv5: e.tensor_copy(wr5[:, k, :, 0, :], wv5[:, k, :, 1, :]), reads=[w.bs[k]], writes=[wr.b])
                S.op(eng, lambda e, k=k, wr5=wr5, wv5=wv5: e.tensor_copy(wr5[:, k, :, 1, :], wv5[:, k, :, 0, :]), reads=[w.bs[k]], writes=[wr.b])
        for s in range(NSEQ):
            with ExitStack() as es2:
                posi = sbt(K, es2, "posi", [128, SL], I32)
                u = sbt(K, es2, "u", [128, SL], F32)
                ni = sbt(K, es2, "ni", [128, SL], I32)
                S.dma("sp", posi.t[:], K.pos[s].partition_broadcast(128), writes=[posi.b])
                for (tab, ph) in ((sinT, 0.5), (cosT, 0.75)):
                    S.op("dve", lambda e, ph=ph: e.tensor_scalar(u.t[:], posi.t[:], K.cst.t[:, 0:1], ph, ALU.mult, ALU.add),
                         reads=[posi.b, K.cst.b], writes=[u.b])
                    S.op("dve", lambda e: e.tensor_copy(ni.t[:], u.t[:]), reads=[u.b], writes=[ni.b])
                    S.op("dve", lambda e: e.tensor_tensor(u.t[:], u.t[:], ni.t[:], ALU.subtract), reads=[u.b, ni.b], writes=[u.b])
                    S.op("dve", lambda e: e.tensor_scalar(u.t[:], u.t[:], -0.5, None, ALU.add), reads=[u.b], writes=[u.b])
                    S.op("dve", lambda e: e.scalar_tensor_tensor(u.t[:], u.t[:], -0.5, u.t[:], ALU.is_lt, ALU.add), reads=[u.b], writes=[u.b])
                    S.op("act", lambda e, tab=tab: e.activation(tab.t[:], u.t[:], AF.Sin, scale=2.0 * math.pi), reads=[u.b], writes=[tab.b])
                S.op("dve", lambda e: e.tensor_scalar(sinT.t[:], sinT.t[:], K.cst.t[:, 1:2], None, ALU.mult), reads=[sinT.b, K.cst.b], writes=[sinT.b])
                S.barrier()
            with ExitStack() as es3:
                xt = [sbt(K, es3, f"xt{i}", [128, D], F32) for i in range(2)]
                ss = [sbt(K, es3, f"ss{i}", [128, 2], F32) for i in range(2)]
                junk = sbt(K, es3, "junk", [128, D], BF16)
                xs = sbt(K, es3, "xs", [128, D], BF16)
                xT2 = [sbt(K, es3, f"xT{i}", [128, 8, 512], BF16) for i in range(2)]
                t1 = [sbt(K, es3, f"t1{i}", [128, 512], F32) for i in range(2)]
                t2 = [sbt(K, es3, f"t2{i}", [128, 512], F32) for i in range(2)]
                ob = [sbt(K, es3, f"ob{i}", [128, 8, 512], BF16) for i in range(2)]
                vb = [sbt(K, es3, f"vb{i}", [128, 8, 129], BF16) for i in range(2)]
                for i in range(2):
                    S.op("pool", lambda e, i=i: e.memset(vb[i].t[:], 1.0), writes=[vb[i].b])
                cnt = 0
                vcnt = 0
                def a1_prologue(tb):
                    xT_ = xT2[tb % 2]
                    for i in range(4):
                        sl = i % 2
                        r0 = s * SL + tb * 512 + i * 128
                        S.dma("sp", xt[sl].t[:], src[r0:r0 + 128, :], writes=[xt[sl].b])
                        norm_xs(K, xt[sl], gw, junk, ss[sl], xs)
                        transpose8(K, xs, 4, xT_.t[:, :, i * 128:(i + 1) * 128], [xT_.b], "act")

                a1_prologue(0)
                for tb in range(SL // 512):
                    xT = xT2[tb % 2]
                    for qi, (w, wr, dram) in enumerate(((wq, wqr, K.QT), (wk, wkr, K.KT))):
                        o_ = ob[qi]
                        for hh in range(NH):
                            if qi == 1 and hh == 0 and tb + 1 < SL // 512:
                                a1_prologue(tb + 1)
                            p = cnt % 2
                            cnt += 1
                            for (ww, pbi) in ((w, 2 * p), (wr, 2 * p + 1)):
                                for k in range(8):
                                    S.op("pe", lambda e, k=k, ww=ww, pbi=pbi: e.matmul(
                                        bank(K, pbi), ww.t[:, k, hh * 128:(hh + 1) * 128], xT.t[:, k, :],
                                        start=(k == 0), stop=(k == 7)),
                                        reads=[xT.b] + ww.bs, writes=[K.pb[pbi]])
                            cs = slice(tb * 512, (tb + 1) * 512)
                            S.op("dve", lambda e: e.tensor_tensor(t1[p].t[:], bank(K, 2 * p), cosT.t[:, cs], ALU.mult),
                                 reads=[K.pb[2 * p], cosT.b], writes=[t1[p].b])
                            S.op("dve", lambda e: e.tensor_tensor(t2[p].t[:], bank(K, 2 * p + 1), sinT.t[:, cs], ALU.mult),
                                 reads=[K.pb[2 * p + 1], sinT.b], writes=[t2[p].b])
                            S.op("pool", lambda e: e.tensor_tensor(o_.t[:, hh, :], t1[p].t[:], t2[p].t[:], ALU.add),
                                 reads=[t1[p].b, t2[p].b], writes=[o_.b])
                        S.dma("pool", dram[s].rearrange("h p s -> p h s")[:, :, tb * 512:(tb + 1) * 512], o_.t[:],
                              reads=[o_.b], writes=[K.scrb[(qi, s, tb)]], ring="st")
                    for i in range(4):
                        v_ = vb[vcnt % 2]
                        vcnt += 1
                        for half in range(2):
                            for k in range(8):
                                S.op("pe", lambda e, k=k, half=half: e.matmul(
                                    bank(K, 5 + half), xT.t[:, k, i * 128:(i + 1) * 128], wv.t[:, k, half * 512:(half + 1) * 512],
                                    start=(k == 0), stop=(k == 7)),
                                    reads=[xT.b] + wv.bs, writes=[K.pb[5 + half]])
                        S.op("act", lambda e: e.copy(v_.t[:, :, 0:128], bank(K, 5, 2).rearrange("p (h v) -> p h v", h=8)),
                             reads=[K.pb[5], K.pb[6]], writes=[v_.b])
                        r0 = s * SL + tb * 512 + i * 128
                        S.dma("pool", K.VA[r0:r0 + 128, :], v_.t[:].rearrange("p h v -> p (h v)"),
                              reads=[v_.b], writes=[K.scrb[(2, r0 // 128)]], ring="st")
                S.barrier()
        S.barrier()


def attn_a2(K, l, src, dst):
    nc, S = K.nc, K.S
    j = l // 2
    SL, NSEQ, T = K.SL, K.NSEQ, K.T
    lam_init = 0.8 - 0.6 * math.exp(-0.3 * l)
    with ExitStack() as es:
        wo = sbt(K, es, "wo", [128, 8, D], BF16, 8)
        KTs = sbt(K, es, "KTs", [128, 8, SL], BF16, 8)
        VAs = sbt(K, es, "VAs", [128, T, 1032], BF16, T)
        swb = sbt(K, es, "swb", [128, 128], F32)
        lv = sbt(K, es, "lv", [128, 4, 64], F32)
        lsum = sbt(K, es, "lsum", [128, 4], F32)
        lam8 = sbt(K, es, "lam8", [128, 4, 2], F32)
        QTb = [sbt(K, es, f"QTb{i}", [128, 8, 512], BF16) for i in range(2)]
        PT = [sbt(K, es, f"PT{i}", [128, 2, 512], BF16) for i in range(4)]
        xt = [sbt(K, es, f"xt{i}", [128, D], F32) for i in range(2)]
        on = sbt(K, es, "on", [128, 4, D], BF16, 4)
        onT = sbt(K, es, "onT", [128, 8, 128], BF16)
        r8 = sbt(K, es, "r8", [128, 4, 2], F32)
        od = sbt(K, es, "od", [128, 4, 128], F32)
        tm = sbt(K, es, "tm", [128, 4, 128], F32)
        sq4 = sbt(K, es, "sq4", [128, 8], F32)
        xo = [sbt(K, es, f"xo{i}", [128, D], F32) for i in range(2)]

        load_w(K, wo, K.attn_w_out[j], 8, 2)
        S.dma("sp", swb.t[:], K.attn_subln_w[j].partition_broadcast(128), writes=[swb.b])
        S.op("dve", lambda e: e.tensor_scalar(swb.t[:], swb.t[:], 1.0 - lam_init, None, ALU.mult), reads=[swb.b], writes=[swb.b])
        for i, a in enumerate((K.attn_lambda_q1, K.attn_lambda_k1, K.attn_lambda_q2, K.attn_lambda_k2)):
            S.dma("sp", lv.t[:, i, :], a[j].partition_broadcast(128), writes=[lv.b])
        S.op("dve", lambda e: e.tensor_tensor(lv.t[:, 0, :], lv.t[:, 0, :], lv.t[:, 1, :], ALU.mult), reads=[lv.b], writes=[lv.b])
        S.op("dve", lambda e: e.tensor_tensor(lv.t[:, 2, :], lv.t[:, 2, :], lv.t[:, 3, :], ALU.mult), reads=[lv.b], writes=[lv.b])
        S.op("dve", lambda e: e.tensor_reduce(lsum.t[:, 0:1], lv.t[:, 0, :], AX.X, ALU.add), reads=[lv.b], writes=[lsum.b])
        S.op("dve", lambda e: e.tensor_reduce(lsum.t[:, 1:2], lv.t[:, 2, :], AX.X, ALU.add), reads=[lv.b], writes=[lsum.b])
        S.op("act", lambda e: e.activation(lsum.t[:, 0:2], lsum.t[:, 0:2], AF.Exp), reads=[lsum.b], writes=[lsum.b])
        S.op("dve", lambda e: e.tensor_tensor(lsum.t[:, 2:3], lsum.t[:, 1:2], lsum.t[:, 0:1], ALU.subtract), reads=[lsum.b], writes=[lsum.b])
        S.op("dve", lambda e: e.tensor_scalar(lsum.t[:, 2:3], lsum.t[:, 2:3], -lam_init, None, ALU.add), reads=[lsum.b], writes=[lsum.b])
        S.op("dve", lambda e: e.memset(lam8.t[:], 1.0), writes=[lam8.b])
        for jj in range(4):
            S.op("dve", lambda e, jj=jj: e.tensor_copy(lam8.t[:, jj, 1:2], lsum.t[:, 2:3]), reads=[lsum.b], writes=[lam8.b])

        accv = K.ps[:, 2048:4096].rearrange("p (j c e) -> p j c e", j=4, c=2)
        accb = [K.pb[4 + jj] for jj in range(4)]
        pcnt = 0
        for s in range(NSEQ):
            for hh in range(NH):
                S.dma("sp", KTs.t[:, hh, :], K.KT[s, hh], reads=[K.scrb[(1, s, tb)] for tb in range(SL // 512)], writes=[KTs.bs[hh]])
            g4 = max(1, T // 4)
            for t0 in range(0, T, g4):
                S.dma("sp", VAs.t[:, t0:t0 + g4, :],
                      K.VA[s * SL + t0 * 128: s * SL + (t0 + g4) * 128, :].rearrange("(t p) c -> p t c", p=128),
                      reads=[K.scrb[(2, s * T + tt)] for tt in range(t0, t0 + g4)], writes=VAs.bs[t0:t0 + g4])
            units = [(qb, hh, kt) for qb in range(SL // 512) for hh in range(NH) for kt in range(4 * qb + 4)]
            ust = {}

            def emit_S(ui):
                nonlocal pcnt
                qb, hh, kt = units[ui]
                Q = QTb[qb % 2]
                if hh == 0 and kt == 0:
                    S.dma("sp", Q.t[:], K.QT[s].rearrange("h p s -> p h s")[:, :, qb * 512:(qb + 1) * 512], reads=[K.scrb[(0, s, qb)]], writes=[Q.b])
                di = kt - 4 * qb
                qlo = max(0, di) * 128
                p = pcnt % 2
                P_ = PT[pcnt % 4]
                pcnt += 1
                ust[ui] = (P_, di, qlo)
                for c in range(2):
                    S.op("pe", lambda e, c=c: e.matmul(
                        bank(K, 2 * p + c)[:, qlo:512], KTs.t[c * 64:(c + 1) * 64, hh, kt * 128:(kt + 1) * 128],
                        Q.t[c * 64:(c + 1) * 64, hh, qlo:512], start=True, stop=True),
                        reads=[KTs.bs[hh], Q.b], writes=[K.pb[2 * p + c]])
                sv = bank(K, 2 * p, 2).rearrange("p (c q) -> p c q", c=2)
                S.op("act", lambda e: e.activation(P_.t[:, :, qlo:512], sv[:, :, qlo:512], AF.Exp, scale=0.125),
                     reads=[K.pb[2 * p], K.pb[2 * p + 1]], writes=[P_.b])
                if di >= 0:
                    S.op("pool", lambda e: e.tensor_tensor(P_.t[:, :, qlo:qlo + 128], P_.t[:, :, qlo:qlo + 128], K.maskc2.t[:], ALU.mult),
                         reads=[P_.b, K.maskc2.b], writes=[P_.b])

            def emit_AV(ui):
                qb, hh, kt = units[ui]
                P_, di, qlo = ust.pop(ui)
                for c in range(2):
                    for jj in range(max(0, di), 4):
                        S.op("pe", lambda e, c=c, jj=jj: e.matmul(
                            accv[:, jj, c, 0:129], P_.t[:, c, jj * 128:(jj + 1) * 128],
                            VAs.t[:, kt, hh * 129:(hh + 1) * 129],
                            start=(kt == 0 and c == 0), stop=(kt == 4 * qb + jj), skip_group_check=True),
                            reads=[P_.b, VAs.bs[kt]], writes=[accb[jj]])

            def post_head(hh):
                S.op("dve", lambda e: e.reciprocal(r8.t[:], accv[:, :, :, 128]), reads=accb, writes=[r8.b])
                S.op("dve", lambda e: e.tensor_tensor(r8.t[:], r8.t[:], lam8.t[:], ALU.mult), reads=[r8.b, lam8.b], writes=[r8.b])
                S.op("dve", lambda e: e.tensor_tensor(od.t[:], accv[:, :, 0, 0:128], r8.t[:, :, 0:1].to_broadcast([128, 4, 128]), ALU.mult),
                     reads=accb + [r8.b], writes=[od.b])
                S.op("dve", lambda e: e.tensor_tensor(tm.t[:], accv[:, :, 1, 0:128], r8.t[:, :, 1:2].to_broadcast([128, 4, 128]), ALU.mult),
                     reads=accb + [r8.b], writes=[tm.b])
                S.op("pool", lambda e: e.tensor_tensor(od.t[:], od.t[:], tm.t[:], ALU.add), reads=[od.b, tm.b], writes=[od.b])
                S.op("pool", lambda e: e.tensor_tensor(tm.t[:], od.t[:], od.t[:], ALU.mult), reads=[od.b], writes=[tm.b])
                S.op("dve", lambda e: e.tensor_reduce(sq4.t[:, 0:4], tm.t[:], AX.X, ALU.add), reads=[tm.b], writes=[sq4.b])
                rstd_ops(K, sq4.t[:, 0:4], sq4.t[:, 4:8], 4, 1.0 / 128, SUBLN_EPS, [sq4.b], [sq4.b])
                S.op("dve", lambda e: e.tensor_tensor(tm.t[:], od.t[:], sq4.t[:, 4:8].unsqueeze(2).to_broadcast([128, 4, 128]), ALU.mult),
                     reads=[od.b, sq4.b], writes=[tm.b])
                S.op("pool", lambda e: e.tensor_tensor(on.t[:, :, hh * 128:(hh + 1) * 128], tm.t[:],
                                                       swb.t[:].unsqueeze(1).to_broadcast([128, 4, 128]), ALU.mult),
                     reads=[tm.b, swb.b], writes=on.bs)

            def out_proj(qb):
                nonlocal pcnt
                for jj in range(4):
                    r0 = s * SL + qb * 512 + jj * 128
                    S.dma("sp", xt[jj % 2].t[:], src[r0:r0 + 128, :], reads=[K.xbuf[r0 // 128]], writes=[xt[jj % 2].b])
                    p = pcnt % 2
                    pcnt += 1
                    pT = bank(K, 2 * p).bitcast(BF16)
                    for k in range(8):
                        S.op("pe", lambda e, k=k: e.transpose(pT[:, k * 128:(k + 1) * 128], on.t[:, jj, k * 128:(k + 1) * 128], K.idb.t[:]),
                             reads=[on.bs[jj], K.idb.b], writes=[K.pb[2 * p]])
                    S.op("dve", lambda e: e.tensor_copy(onT.t[:], pT[:, 0:1024].rearrange("p (k t) -> p k t", k=8)), reads=[K.pb[2 * p]], writes=[onT.b])
                    p = pcnt % 2
                    pcnt += 1
                    for half in range(2):
                        for k in range(8):
                            S.op("pe", lambda e, k=k, half=half: e.matmul(
                                bank(K, 2 * p + half), onT.t[:, k, :], wo.t[:, k, half * 512:(half + 1) * 512],
                                start=(k == 0), stop=(k == 7)),
                                reads=[onT.b] + wo.bs, writes=[K.pb[2 * p + half]])
                    x_ = xo[jj % 2]
                    S.op("dve", lambda e: e.tensor_tensor(x_.t[:], bank(K, 2 * p, 2), xt[jj % 2].t[:], ALU.add),
                         reads=[K.pb[2 * p], K.pb[2 * p + 1], xt[jj % 2].b], writes=[x_.b])
                    S.dma("pool", dst[r0:r0 + 128, :], x_.t[:], reads=[x_.b], writes=[K.xbuf[r0 // 128]], ring="st")

            emit_S(0)
            for ui, (qb, hh, kt) in enumerate(units):
                if ui + 1 < len(units):
                    emit_S(ui + 1)
                emit_AV(ui)
                if kt == 4 * qb + 3:
                    post_head(hh)
                    if hh == NH - 1:
                        out_proj(qb)
        S.barrier()


def hgrn_phase(K, l, src, dst):
    nc, S = K.nc, K.S
    j = l // 2
    SL, NSEQ, T = K.SL, K.NSEQ, K.T
    with ExitStack() as es:
        lbb = sbt(K, es, "lbb", [128, D], F32)
        with ExitStack() as es0:
            lp = sbt(K, es0, "lp", [128, 4, D], F32)
            den = sbt(K, es0, "den", [128, D], F32)
            for i in range(4):
                S.dma("sp", lp.t[:, i, :], K.hgrn_lb_param[i].partition_broadcast(128), writes=[lp.b])
            S.op("act", lambda e: e.activation(lp.t[:], lp.t[:], AF.Exp), reads=[lp.b], writes=[lp.b])
            S.op("dve", lambda e: e.tensor_tensor(den.t[:], lp.t[:, 0, :], lp.t[:, 1, :], ALU.add), reads=[lp.b], writes=[den.b])
            S.op("dve", lambda e: e.tensor_tensor(den.t[:], den.t[:], lp.t[:, 2, :], ALU.add), reads=[lp.b, den.b], writes=[den.b])
            S.op("dve", lambda e: e.tensor_tensor(den.t[:], den.t[:], lp.t[:, 3, :], ALU.add), reads=[lp.b, den.b], writes=[den.b])
            S.op("dve", lambda e: e.reciprocal(den.t[:], den.t[:]), reads=[den.b], writes=[den.b])
            S.op("dve", lambda e: e.memset(lbb.t[:], 0.0), writes=[lbb.b])
            for i in range(1, l + 1):
                S.op("dve", lambda e, i=i: e.tensor_tensor(lbb.t[:], lbb.t[:], lp.t[:, i, :], ALU.add), reads=[lp.b, lbb.b], writes=[lbb.b])
            S.op("dve", lambda e: e.tensor_tensor(lbb.t[:], lbb.t[:], den.t[:], ALU.mult), reads=[den.b, lbb.b], writes=[lbb.b])
            S.barrier()
        win = sbt(K, es, "win", [128, 8, 4 * D], BF16, 8)
        wout = sbt(K, es, "wout", [128, 8, D], BF16, 8)
        gw = sbt(K, es, "gw", [128, D], F32)
        gnb = sbt(K, es, "gnb", [128, 128], F32)
        xt = [sbt(K, es, f"xt{i}", [128, D], F32) for i in range(2)]
        ss = [sbt(K, es, f"ss{i}", [128, 2], F32) for i in range(2)]
        junk = sbt(K, es, "junk", [128, D], BF16)
        xs = sbt(K, es, "xs", [128, D], BF16)
        xT = sbt(K, es, "xT", [128, 8, 128], BF16)
        E = [sbt(K, es, f"E{i}", [128, D], F32) for i in range(4)]
        Ebg = [[sbt(K, es, f"Eb{i}{g}", [128, 512], F32) for g in range(2)] for i in range(2)]
        sq8g = [sbt(K, es, f"sq8g{g}", [128, 8], F32) for g in range(2)]
        logf = sbt(K, es, "logf", [128, D], F32)
        sgw2 = [sbt(K, es, f"sgw{i}", [128, D], F32) for i in range(2)]
        kd2 = [sbt(K, es, f"kd{i}", [128, D], BF16) for i in range(2)]
        qd2 = [sbt(K, es, f"qd{i}", [128, D], BF16) for i in range(2)]
        vv2 = [sbt(K, es, f"vv{i}", [128, D], BF16) for i in range(2)]
        scr2 = [sbt(K, es, f"scr{i}", [128, 8], F32) for i in range(2)]
        sb2 = [sbt(K, es, f"sb{i}", [128, 8], F32) for i in range(2)]
        onb = sbt(K, es, "onb", [128, D], BF16)
        qdTa = sbt(K, es, "qdTa", [128, 8, 128], BF16)
        qdTb = sbt(K, es, "qdTb", [128, 8, 128], BF16)
        kdT = sbt(K, es, "kdT", [128, 8, 128], BF16)
        Am = sbt(K, es, "Am", [128, 8, 128], BF16, 2)
        onT = sbt(K, es, "onT", [128, 8, 128], BF16)
        W = sbt(K, es, "W", [128, 8, 128], F32, 8)
        R0 = sbt(K, es, "R0", [128, 8, 128], BF16)
        R1 = sbt(K, es, "R1", [128, 8, 128], BF16, 2)
        dcur = sbt(K, es, "dcur", [128, 8, 3], F32)
        cprev = sbt(K, es, "cprev", [128, 8], F32)
        sq8 = sbt(K, es, "sq8", [128, 16], F32)
        xo = [sbt(K, es, f"xo{i}", [128, D], F32) for i in range(2)]

        S.dma("sp", gw.t[:], K.norm_mix_w[l].partition_broadcast(128), writes=[gw.b])
        S.dma("sp", gnb.t[:], K.hgrn_gnorm_w[j].partition_broadcast(128), writes=[gnb.b])
        load_w(K, win, K.hgrn_w_in[j], 8, 1)
        load_w(K, wout, K.hgrn_w_out[j], 8, 2)
        S.op("pool", lambda e: e.memset(qdTa.t[:], 0.0), writes=[qdTa.b])
        S.op("pool", lambda e: e.memset(qdTb.t[:], 0.0), writes=[qdTb.b])
        one = K.epsb[1.0]

        def proj(qi):
            p = next_pair(K)
            for half in range(2):
                for k in range(8):
                    S.op("pe", lambda e, k=k, half=half: e.matmul(
                        bank(K, 2 * p + half), xT.t[:, k, :], win.t[:, k, qi * D + half * 512: qi * D + (half + 1) * 512],
                        start=(k == 0), stop=(k == 7)),
                        reads=[xT.b, win.bs[k]], writes=[K.pb[2 * p + half]])
            return bank(K, 2 * p, 2), [K.pb[2 * p], K.pb[2 * p + 1]]

        def sig_chain(src_ps, src_b, A, Bq):
            S.op("act", lambda e: e.activation(A.t[:], src_ps, AF.Exp, scale=-1.0), reads=src_b, writes=[A.b])
            S.op("act", lambda e: e.activation(Bq.t[:], A.t[:], AF.Ln, bias=one.t[:, 0:1], scale=1.0), reads=[A.b, one.b], writes=[Bq.b])
            S.op("act", lambda e: e.activation(A.t[:], Bq.t[:], AF.Exp, scale=-1.0), reads=[Bq.b], writes=[A.b])

        def A1(t):
            sl = t % 2
            tl = t % T
            kd, scr_, sb_ = kd2[sl], scr2[sl], sb2[sl]
            if tl == 0:
                S.op("dve", lambda e: e.memset(cprev.t[:], 0.0), writes=[cprev.b])
            S.dma("sp", xt[sl].t[:], src[t * 128:(t + 1) * 128, :], reads=[K.xbuf[t]], writes=[xt[sl].b])
            norm_xs(K, xt[sl], gw, junk, ss[sl], xs)
            p = next_pair(K)
            transpose8(K, xs, 2 * p, xT.t[:], [xT.b], "act")
            fz, fzb = proj(1)
            S.op("act", lambda e: e.activation(E[0].t[:], fz, AF.Exp, scale=-1.0), reads=fzb, writes=[E[0].b])
            S.op("dve", lambda e: e.tensor_tensor(E[1].t[:], E[0].t[:], lbb.t[:], ALU.mult), reads=[E[0].b, lbb.b], writes=[E[1].b])
            S.op("act", lambda e: e.activation(E[1].t[:], E[1].t[:], AF.Ln, bias=one.t[:, 0:1], scale=1.0), reads=[E[1].b, one.b], writes=[E[1].b])
            S.op("act", lambda e: e.activation(E[0].t[:], E[0].t[:], AF.Ln, bias=one.t[:, 0:1], scale=1.0), reads=[E[0].b, one.b], writes=[E[0].b])
            S.op("dve", lambda e: e.tensor_tensor(logf.t[:], E[1].t[:], E[0].t[:], ALU.subtract), reads=[E[0].b, E[1].b], writes=[logf.b])
            p = next_pair(K)
            for half in range(2):
                S.op("pe", lambda e, half=half: e.matmul(bank(K, 2 * p + half), K.Tm.t[:], logf.t[:, half * 512:(half + 1) * 512], start=True, stop=True),
                     reads=[K.Tm.b, logf.b], writes=[K.pb[2 * p + half]])
            brel, brb = bank(K, 2 * p, 2), [K.pb[2 * p], K.pb[2 * p + 1]]
            pd = next_pair(K)
            dps = bank(K, 2 * pd)[:, 0:24].rearrange("p (h c) -> p h c", c=3)
            for hh in range(NH):
                S.op("pe", lambda e, hh=hh: e.matmul(dps[:, hh, :], logf.t[:, hh * 128:(hh + 1) * 128], K.Sel.t[:], start=True, stop=True),
                     reads=[logf.b, K.Sel.b], writes=[K.pb[2 * pd]])
            S.op("dve", lambda e: e.tensor_copy(dcur.t[:], dps), reads=[K.pb[2 * pd]], writes=[dcur.b])
            S.op("act", lambda e: e.activation(E[2].t[:], brel, AF.Exp), reads=brb, writes=[E[2].b])
            S.op("act", lambda e: e.activation(E[3].t[:], brel, AF.Exp, scale=-1.0), reads=brb, writes=[E[3].b])
            S.op("act", lambda e: e.activation(E[0].t[:], logf.t[:], AF.Exp), reads=[logf.b], writes=[E[0].b])
            S.op("dve", lambda e: e.tensor_scalar(E[0].t[:], E[0].t[:], -1.0, 1.0, ALU.mult, ALU.add), reads=[E[0].b], writes=[E[0].b])
            S.op("dve", lambda e: e.tensor_tensor(kd.t[:], E[0].t[:], E[3].t[:], ALU.mult), reads=[E[0].b, E[3].b], writes=[kd.b])
            S.op("dve", lambda e: e.tensor_tensor(scr_.t[:], cprev.t[:], dcur.t[:, :, 0], ALU.add), reads=[cprev.b, dcur.b], writes=[scr_.b])
            S.op("act", lambda e: e.activation(scr_.t[:], scr_.t[:], AF.Exp), reads=[scr_.b], writes=[scr_.b])
            S.op("act", lambda e: e.activation(sb_.t[:], dcur.t[:, :, 1], AF.Exp), reads=[dcur.b], writes=[sb_.b])
            S.op("dve", lambda e: e.tensor_copy(cprev.t[:], dcur.t[:, :, 2]), reads=[dcur.b], writes=[cprev.b])

        def A2(t):
            sl = t % 2
            qd = qd2[sl]
            q_, qb_ = proj(0)
            sig_chain(q_, qb_, E[0], E[1])
            S.op("dve", lambda e: e.tensor_tensor(E[1].t[:], q_, E[0].t[:], ALU.mult), reads=qb_ + [E[0].b], writes=[E[1].b])
            S.op("dve", lambda e: e.tensor_tensor(qd.t[:], E[1].t[:], E[2].t[:], ALU.mult), reads=[E[1].b, E[2].b], writes=[qd.b])

        def A3(t):
            sl = t % 2
            vv, sgw = vv2[sl], sgw2[sl]
            i_, ib_ = proj(2)
            S.op("act", lambda e: e.copy(vv.t[:], i_), reads=ib_, writes=[vv.b])
            g_, gb_ = proj(3)
            sig_chain(g_, gb_, E[0], E[1])
            S.op("dve", lambda e: e.tensor_tensor(E[1].t[:], g_, E[0].t[:], ALU.mult), reads=gb_ + [E[0].b], writes=[E[1].b])
            S.op("pool", lambda e: e.tensor_tensor(sgw.t[:].rearrange("p (h v) -> p h v", h=8), E[1].t[:].rearrange("p (h v) -> p h v", h=8),
                                                   gnb.t[:].unsqueeze(1).to_broadcast([128, 8, 128]), ALU.mult),
                 reads=[E[1].b, gnb.b], writes=[sgw.b])

        at = bank(K, 5).rearrange("p (h c) -> p h c", h=4)
        m0 = bank(K, 6).rearrange("p (h c) -> p h c", h=4)
        m1 = bank(K, 7).rearrange("p (h c) -> p h c", h=4)
        bst = {}

        def B0(t):
            sl = t % 2
            tl = t % T
            kd, qd, scr_ = kd2[sl], qd2[sl], scr2[sl]
            if tl == 0:
                S.op("dve", lambda e: e.memset(W.t[:], 0.0), writes=W.bs)
            S.op("pool", lambda e: e.tensor_tensor(R0.t[:], W.t[:], scr_.t[:].unsqueeze(2).to_broadcast([128, 8, 128]), ALU.mult),
                 reads=W.bs + [scr_.b], writes=[R0.b])
            p = next_pair(K)
            pT = bank(K, 2 * p).bitcast(BF16)
            for k in range(8):
                S.op("pe", lambda e, k=k: e.transpose(pT[:, k * 128:(k + 1) * 128], qd.t[:, k * 128:(k + 1) * 128], K.idb.t[:]),
                     reads=[qd.b, K.idb.b], writes=[K.pb[2 * p]])
            pv = pT[:, 0:1024].rearrange("p (k t) -> p k t", k=8)
            S.op("act", lambda e: e.copy(qdTa.t[:, :, 0:64], pv[:, :, 0:64]), reads=[K.pb[2 * p]], writes=[qdTa.b])
            S.op("dve", lambda e: e.tensor_copy(qdTb.t[:, :, 64:128], pv[:, :, 64:128]), reads=[K.pb[2 * p]], writes=[qdTb.b])
            p = next_pair(K)
            transpose8(K, kd, 2 * p, kdT.t[:], [kdT.b], "dve")

        def Bg_mm(t, hg):
            sl = t % 2
            kd, vv = kd2[sl], vv2[sl]
            hs = range(4 * hg, 4 * hg + 4)
            for hh in hs:
                S.op("pe", lambda e, hh=hh: e.matmul(at[:, hh % 4, 0:64], kdT.t[:, hh, :], qdTa.t[:, hh, 0:64], start=True, stop=True),
                     reads=[kdT.b, qdTa.b], writes=[K.pb[5]])
                S.op("pe", lambda e, hh=hh: e.matmul(at[:, hh % 4, 64:128], kdT.t[:, hh, :], qdTb.t[:, hh, 64:128], start=True, stop=True),
                     reads=[kdT.b, qdTb.b], writes=[K.pb[5]])
            for hh in hs:
                S.op("pe", lambda e, hh=hh: e.matmul(m0[:, hh % 4, :], kd.t[0:64, hh * 128:(hh + 1) * 128], vv.t[0:64, hh * 128:(hh + 1) * 128], start=True, stop=True),
                     reads=[kd.b, vv.b], writes=[K.pb[6]])
            for hh in hs:
                S.op("pe", lambda e, hh=hh: e.matmul(m1[:, hh % 4, :], kd.t[64:128, hh * 128:(hh + 1) * 128], vv.t[64:128, hh * 128:(hh + 1) * 128], start=True, stop=True),
                     reads=[kd.b, vv.b], writes=[K.pb[7]])
            S.op("dve", lambda e: e.tensor_tensor(Am.t[:, 4 * hg:4 * hg + 4, :], at, K.mask64.t[:].unsqueeze(1).to_broadcast([128, 4, 128]), ALU.mult),
                 reads=[K.pb[5], K.mask64.b], writes=[Am.bs[hg]])

        def Bg_state(t, hg):
            sl = t % 2
            scr_, sb_, vv = scr2[sl], sb2[sl], vv2[sl]
            o4 = bank(K, 4).rearrange("p (h v) -> p h v", h=4)
            hs = range(4 * hg, 4 * hg + 4)
            for hh in hs:
                S.op("dve", lambda e, hh=hh: e.scalar_tensor_tensor(W.t[:, hh, :], W.t[:, hh, :], scr_.t[:, hh:hh + 1], m0[:, hh % 4, :], ALU.mult, ALU.add),
                     reads=[W.bs[hh], scr_.b, K.pb[6], R0.b], writes=[W.bs[hh]])
            S.op("pool", lambda e: e.tensor_tensor(R1.t[:, 4 * hg:4 * hg + 4, :], W.t[:, 4 * hg:4 * hg + 4, :],
                                                   sb_.t[:, 4 * hg:4 * hg + 4].unsqueeze(2).to_broadcast([128, 4, 128]), ALU.mult),
                 reads=W.bs[4 * hg:4 * hg + 4] + [sb_.b], writes=[R1.bs[hg]])
            for hh in hs:
                oh = o4[:, hh % 4, :]
                pbo = K.pb[4]
                S.op("pe", lambda e, hh=hh, oh=oh: e.matmul(oh, Am.t[:, hh, :], vv.t[:, hh * 128:(hh + 1) * 128], start=True, stop=False),
                     reads=[Am.bs[hg], vv.b], writes=[pbo])
                S.op("pe", lambda e, hh=hh, oh=oh: e.matmul(oh, qdTa.t[:, hh, :], R0.t[:, hh, :], start=False, stop=False),
                     reads=[qdTa.b, R0.b], writes=[pbo])
                S.op("pe", lambda e, hh=hh, oh=oh: e.matmul(oh, qdTb.t[:, hh, :], R1.t[:, hh, :], start=False, stop=True),
                     reads=[qdTb.b, R1.bs[hg]], writes=[pbo])
            for hh in hs:
                S.op("dve", lambda e, hh=hh: e.scalar_tensor_tensor(W.t[:, hh, :], W.t[:, hh, :], sb_.t[:, hh:hh + 1], m1[:, hh % 4, :], ALU.mult, ALU.add),
                     reads=[W.bs[hh], sb_.b, K.pb[7], R1.bs[hg]], writes=[W.bs[hh]])
            sgw = sgw2[sl]
            cs = slice(512 * hg, 512 * hg + 512)
            e0, e1, sq = Ebg[0][hg], Ebg[1][hg], sq8g[hg]
            S.op("act", lambda e: e.activation(e0.t[:], bank(K, 4), AF.Square), reads=[K.pb[4]], writes=[e0.b])
            S.op("dve", lambda e: e.tensor_reduce(sq.t[:, 0:4], e0.t[:].rearrange("p (h v) -> p h v", h=4), AX.X, ALU.add), reads=[e0.b], writes=[sq.b])
            rstd_ops(K, sq.t[:, 0:4], sq.t[:, 4:8], 4, 1.0 / 128, GN_EPS, [sq.b], [sq.b])
            S.op("dve", lambda e: e.tensor_tensor(e1.t[:].rearrange("p (h v) -> p h v", h=4), o4,
                                                  sq.t[:, 4:8].unsqueeze(2).to_broadcast([128, 4, 128]), ALU.mult),
                 reads=[K.pb[4], sq.b], writes=[e1.b])
            S.op("pool", lambda e: e.tensor_tensor(onb.t[:, cs], e1.t[:], sgw.t[:, cs], ALU.mult), reads=[e1.b, sgw.b], writes=[onb.b])

        def B3(t):
            sl = t % 2
            p = next_pair(K)
            transpose8(K, onb, 2 * p, onT.t[:], [onT.b], "act")
            p = next_pair(K)
            for half in range(2):
                for k in range(8):
                    S.op("pe", lambda e, k=k, half=half: e.matmul(
                        bank(K, 2 * p + half), onT.t[:, k, :], wout.t[:, k, half * 512:(half + 1) * 512],
                        start=(k == 0), stop=(k == 7)),
                        reads=[onT.b] + wout.bs, writes=[K.pb[2 * p + half]])
            S.op("dve", lambda e: e.tensor_tensor(xo[sl].t[:], bank(K, 2 * p, 2), xt[sl].t[:], ALU.add),
                 reads=[K.pb[2 * p], K.pb[2 * p + 1], xt[sl].b], writes=[xo[sl].b])
            S.dma("pool", dst[t * 128:(t + 1) * 128, :], xo[sl].t[:], reads=[xo[sl].b], writes=[K.xbuf[t]], ring="st")

        NTT = NSEQ * T
        A1(0); A2(0); A3(0)
        for t in range(NTT):
            nx = t + 1 < NTT
            B0(t)
            Bg_mm(t, 0)
            if nx:
                A1(t + 1)
            Bg_state(t, 0)
            Bg_mm(t, 1)
            if nx:
                A2(t + 1)
            Bg_state(t, 1)
            B3(t)
            if nx:
                A3(t + 1)
        S.barrier()


def host_consts():
    ident = np.eye(128, dtype=np.float32)
    kk = np.arange(128)[:, None]
    qq = np.arange(128)[None, :]
    maskc = (kk <= qq).astype(np.float32)
    mask64 = ((kk <= qq) & ((kk // 64) == (qq // 64))).astype(np.float32)
    Tm = np.zeros((128, 128), np.float32)
    for c in range(128):
        base = (c // 64) * 64
        ref = base + 31
        if c > ref:
            Tm[ref + 1:c + 1, c] = 1.0
        elif c < ref:
            Tm[c + 1:ref + 1, c] = -1.0
    Sel = np.zeros((128, 3), np.float32)
    Sel[0:32, 0] = 1.0
    Sel[32:96, 1] = 1.0
    Sel[96:128, 2] = 1.0
    p = np.arange(128)
    invf = 1.0 / (10000.0 ** ((p % 32).astype(np.float64) / 32.0))
    cst = np.zeros((128, 4), np.float32)
    cst[:, 0] = (invf / (2.0 * np.pi)).astype(np.float32)
    cst[:, 1] = np.where((p % 64) < 32, -1.0, 1.0)
    cst[:, 2] = 1.0
    return dict(c_ident=ident, c_maskc=maskc, c_mask64=mask64, c_Tm=Tm, c_Sel=Sel, c_cst=cst)


WSPECS = [
    ("norm_mix_w", [4, D]), ("norm_ffn_w", [4, D]), ("final_norm_w", [D]),
    ("attn_w_in", [2, D, 3 * D]), ("attn_w_out", [2, D, D]),
    ("attn_lambda_q1", [2, 64]), ("attn_lambda_k1", [2, 64]), ("attn_lambda_q2", [2, 64]), ("attn_lambda_k2", [2, 64]),
    ("attn_subln_w", [2, 128]), ("hgrn_w_in", [2, D, 4 * D]), ("hgrn_w_out", [2, D, D]),
    ("hgrn_gnorm_w", [2, 128]), ("hgrn_lb_param", [4, D]),
    ("ffn_w_in", [4, D, 2 * FH]), ("ffn_w_out", [4, FH, D]),
]
CSPECS = [("c_ident", [128, 128]), ("c_maskc", [128, 128]), ("c_mask64", [128, 128]), ("c_Tm", [128, 128]),
          ("c_Sel", [128, 3]), ("c_cst", [128, 4])]


def build(NSEQ=2, SL=4096, prog=None):
    if prog is None:
        prog = []
        for l in range(4):
            prog.append(("attn" if l % 2 == 0 else "hgrn", l))
            prog.append(("ffn", l))
    nc = bass.Bass("TRN2", target_bir_lowering=False)
    K = Ctx()
    K.nc = nc
    K.uid = 0
    K.NSEQ, K.SL = NSEQ, SL
    K.T = SL // 128
    K.NT = NSEQ * K.T
    NTOK = NSEQ * SL
    x_in = nc.dram_tensor("x", [NTOK, D], F32, kind="ExternalInput").ap()
    K.pos = nc.dram_tensor("positions", [NSEQ, SL], I32, kind="ExternalInput").ap()
    for name, shp in WSPECS:
        setattr(K, name, nc.dram_tensor(name, shp, F32, kind="ExternalInput").ap())
    cd = {name: nc.dram_tensor(name, shp, F32, kind="ExternalInput").ap() for name, shp in CSPECS}
    out = nc.dram_tensor("out", [NTOK, D], F32, kind="ExternalOutput").ap()
    xres = nc.dram_tensor("xres", [NTOK, D], F32).ap()
    K.QT = nc.dram_tensor("QT", [NSEQ, NH, 128, SL], BF16).ap()
    K.KT = nc.dram_tensor("KT", [NSEQ, NH, 128, SL], BF16).ap()
    K.VA = nc.dram_tensor("VA", [NTOK, 1032], BF16).ap()
    with ExitStack() as es:
        S = Sched(nc, es)
        K.S = S
        K.ps = es.enter_context(nc.psum_tensor("ps", [128, 4096], F32))
        K.pb = [Buf(f"pb{i}") for i in range(8)]
        K.pair_i = 0
        K.xbuf = [Buf() for _ in range(K.NT)]
        K.scrb = {}
        for s_ in range(NSEQ):
            for tb_ in range(SL // 512):
                K.scrb[(0, s_, tb_)] = Buf()
                K.scrb[(1, s_, tb_)] = Buf()
        for t_ in range(K.NT):
            K.scrb[(2, t_)] = Buf()
        K.idb = sbt(K, es, "idb", [128, 128], BF16)
        K.maskc2 = sbt(K, es, "maskc2", [128, 2, 128], BF16)
        K.mask64 = sbt(K, es, "mask64", [128, 128], F32)
        K.Tm = sbt(K, es, "Tm", [128, 128], F32)
        K.Sel = sbt(K, es, "Sel", [128, 3], F32)
        K.cst = sbt(K, es, "cst", [128, 4], F32)
        K.epsb = {}
        for v in (NORM_EPS, SUBLN_EPS, 1.0):
            if v not in K.epsb:
                tl_ = sbt(K, es, f"eps{len(K.epsb)}", [128, 1], F32)
                S.op("dve", lambda e, tl_=tl_, v=v: e.memset(tl_.t[:], v), writes=[tl_.b])
                K.epsb[v] = tl_
        with ExitStack() as es0:
            tmpf = sbt(K, es0, "tmpf", [128, 128], F32)
            S.dma("sp", tmpf.t[:], cd["c_ident"], writes=[tmpf.b])
            S.op("dve", lambda e: e.tensor_copy(K.idb.t[:], tmpf.t[:]), reads=[tmpf.b], writes=[K.idb.b])
            tmpm = sbt(K, es0, "tmpm", [128, 128], F32)
            S.dma("sp", tmpm.t[:], cd["c_maskc"], writes=[tmpm.b])
            for c in range(2):
                S.op("dve", lambda e, c=c: e.tensor_copy(K.maskc2.t[:, c, :], tmpm.t[:]), reads=[tmpm.b], writes=[K.maskc2.b])
            S.dma("sp", K.mask64.t[:], cd["c_mask64"], writes=[K.mask64.b])
            S.dma("sp", K.Tm.t[:], cd["c_Tm"], writes=[K.Tm.b])
            S.dma("sp", K.Sel.t[:], cd["c_Sel"], writes=[K.Sel.b])
            S.dma("sp", K.cst.t[:], cd["c_cst"], writes=[K.cst.b])
            S.barrier()
        cur = x_in
        nph = len(prog)
        for pi, (kind, l) in enumerate(prog):
            last = pi == nph - 1
            dst = out if last else xres
            if kind == "ffn":
                ffn_phase(K, l, cur, dst, final=last)
            elif kind == "attn":
                attn_a1(K, l, cur)
                attn_a2(K, l, cur, dst)
            elif kind == "hgrn":
                hgrn_phase(K, l, cur, dst)
            cur = dst
        S.barrier()
        K.stats = (S.nsem, dict(S.n_inst), dict(S.n_wait))
    return nc, K


_CACHE = {}


def kernel(**inputs):
    x = np.asarray(inputs["x"], np.float32)
    pos = np.asarray(inputs["positions"], np.int32)
    B, SLn, _ = x.shape
    ncore = 8
    nseq = B // ncore
    key = (nseq, SLn)
    if key not in _CACHE:
        _CACHE[key] = build(nseq, SLn)[0]
    nc = _CACHE[key]
    consts = host_consts()
    wts = {name: np.ascontiguousarray(np.asarray(inputs[name], np.float32)) for name, _ in WSPECS}
    in_maps = []
    for c in range(ncore):
        m = {"x": np.ascontiguousarray(x[c * nseq:(c + 1) * nseq].reshape(nseq * SLn, D)),
             "positions": np.ascontiguousarray(pos[c * nseq:(c + 1) * nseq])}
        m.update(wts)
        m.update(consts)
        in_maps.append(m)
    res = run_bass_kernel_spmd(nc, in_maps, core_ids=list(range(ncore)))
    outs = [np.asarray(r["out"], np.float32).reshape(nseq, SLn, D) for r in res.results]
    return np.concatenate(outs, axis=0)
```

```python
import math
from contextlib import ExitStack

import numpy as np
import concourse.bass as bass
import concourse.mybir as mybir
from concourse.bass_utils import run_bass_kernel_spmd

F32 = mybir.dt.float32
BF16 = mybir.dt.bfloat16
I32 = mybir.dt.int32
AF = mybir.ActivationFunctionType
ALU = mybir.AluOpType
AX = mybir.AxisListType

D = 1024
FH = 2816
NH = 8
SEM_LIMIT = 30000
NORM_EPS = 1e-6
SUBLN_EPS = 1e-5
GN_EPS = 1e-6


class Counter:
    def __init__(self, S):
        self.S = S
        self.sem = None
        self.val = 0

    def peek(self):
        return (self.sem, self.val)

    def next(self, inc):
        if self.sem is None or self.val + inc > SEM_LIMIT:
            self.sem = self.S.new_sem()
            self.val = 0
        self.val += inc
        return (self.sem, self.val)


class Buf:
    __slots__ = ("name", "w", "r")

    def __init__(self, name=""):
        self.name = name
        self.w = []
        self.r = []


class Sched:
    def __init__(self, nc, es, same_engine_sync=("act", "dve", "pool")):
        self.nc = nc
        self.es = es
        self.eng = {"pe": nc.tensor, "act": nc.scalar, "dve": nc.vector,
                    "pool": nc.gpsimd, "sp": nc.sync}
        self.ctr = {e: Counter(self) for e in self.eng}
        self.waited = {e: {} for e in self.eng}
        self.same = set(same_engine_sync)
        self.nsem = 0
        self.rings = {}
        self.n_inst = {e: 0 for e in self.eng}
        self.n_wait = {e: 0 for e in self.eng}

    def new_sem(self):
        self.nsem += 1
        return self.es.enter_context(self.nc.semaphore(f"s{self.nsem}"))

    def _wait(self, E, tok):
        sem, val = tok
        if sem is None or val <= 0:
            return
        w = self.waited[E]
        k = id(sem)
        if w.get(k, 0) >= val:
            return
        w[k] = val
        self.eng[E].wait_ge(sem, val)
        self.n_wait[E] += 1

    def _deps(self, E, reads, writes):
        own = self.ctr[E].sem
        for b in reads:
            for t in b.w:
                if t[0] is own and E not in self.same:
                    continue
                self._wait(E, t)
        for b in writes:
            for t in b.w + b.r:
                if t[0] is own and E not in self.same:
                    continue
                self._wait(E, t)

    def _commit(self, tok, reads, writes):
        for b in reads:
            b.r.append(tok)
            if len(b.r) > 32:
                b.r = b.r[-32:]
        for b in writes:
            b.w = [tok]
            b.r = []

    def op(self, E, fn, reads=(), writes=()):
        self._deps(E, reads, writes)
        inst = fn(self.eng[E])
        tok = self.ctr[E].next(1)
        inst.then_inc(tok[0], 1)
        self.n_inst[E] += 1
        self._commit(tok, reads, writes)
        return tok

    def dma(self, Q, out, in_, reads=(), writes=(), ring="ld", nring=8):
        key = (Q, ring)
        if key not in self.rings:
            self.rings[key] = [[Counter(self) for _ in range(nring)], 0]
        rg = self.rings[key]
        c = rg[0][rg[1] % len(rg[0])]
        rg[1] += 1
        self._wait(Q, c.peek())
        self._deps(Q, reads, writes)
        tok = c.next(16)
        self.eng[Q].dma_start(out=out, in_=in_).then_inc(tok[0], 16)
        self.n_inst[Q] += 1
        self._commit(tok, reads, writes)
        return tok

    def barrier(self):
        toks = [c.peek() for c in self.ctr.values()]
        for rg in self.rings.values():
            toks += [c.peek() for c in rg[0]]
        for E in self.eng:
            for t in toks:
                self._wait(E, t)


class Tl:
    def __init__(self, t, n=1):
        self.t = t
        self.bs = [Buf() for _ in range(n)]

    @property
    def b(self):
        return self.bs[0]


class Ctx:
    pass


def sbt(K, es, name, shape, dt, n=1):
    K.uid += 1
    return Tl(es.enter_context(K.nc.sbuf_tensor(f"{name}_{K.uid}", shape, dt)), n)


def bank(K, i, n=1):
    return K.ps[:, i * 512:(i + n) * 512]


def next_pair(K):
    p = K.pair_i % 2
    K.pair_i += 1
    return p


def rstd_ops(K, ss, rs, n, inv_n, eps, rd, wr):
    S = K.S
    S.op("act", lambda e: e.activation(rs, ss, AF.Ln, bias=K.epsb[eps].t[:, 0:1], scale=inv_n), reads=rd + [K.epsb[eps].b], writes=wr)
    S.op("act", lambda e: e.activation(rs, rs, AF.Exp, scale=-0.5), reads=wr, writes=wr)


def norm_xs(K, xt, gw, junk, ss, xs):
    S = K.S
    S.op("act", lambda e: e.activation(junk.t[:], xt.t[:], AF.Square, accum_out=ss.t[:, 0:1]),
         reads=[xt.b], writes=[junk.b, ss.b])
    rstd_ops(K, ss.t[:, 0:1], ss.t[:, 1:2], 1, 1.0 / D, NORM_EPS, [ss.b], [ss.b])
    S.op("dve", lambda e: e.scalar_tensor_tensor(xs.t[:], xt.t[:], ss.t[:, 1:2], gw.t[:], ALU.mult, ALU.mult),
         reads=[xt.b, ss.b, gw.b], writes=[xs.b])


def transpose8(K, src, pbank_i, dst_ap, dst_bufs, eng="act", nblk=8):
    S = K.S
    pT = bank(K, pbank_i).bitcast(BF16)
    for k in range(nblk):
        S.op("pe", lambda e, k=k: e.transpose(pT[:, k * 128:(k + 1) * 128], src.t[:, k * 128:(k + 1) * 128], K.idb.t[:]),
             reads=[src.b, K.idb.b], writes=[K.pb[pbank_i]])
    pv = pT[:, 0:nblk * 128].rearrange("p (k t) -> p k t", k=nblk)
    if eng == "act":
        S.op("act", lambda e: e.copy(dst_ap, pv), reads=[K.pb[pbank_i]], writes=dst_bufs)
    else:
        S.op(eng, lambda e: e.tensor_copy(dst_ap, pv), reads=[K.pb[pbank_i]], writes=dst_bufs)


def load_w(K, wt, src, nk, per=1):
    v = src.rearrange("(k p) n -> p k n", p=128)
    for k0 in range(0, nk, per):
        k1 = min(nk, k0 + per)
        K.S.dma("pool", wt.t[:, k0:k1, :], v[:, k0:k1, :], writes=wt.bs[k0:k1], ring="w")


def silu_sig(K, src_ps, src_buf, ea, eb):
    S = K.S
    S.op("act", lambda e: e.activation(ea.t[:], src_ps, AF.Exp, scale=-1.0), reads=[src_buf], writes=[ea.b])
    S.op("act", lambda e: e.activation(eb.t[:], ea.t[:], AF.Ln, bias=K.epsb[1.0].t[:, 0:1], scale=1.0), reads=[ea.b, K.epsb[1.0].b], writes=[eb.b])
    S.op("act", lambda e: e.activation(ea.t[:], eb.t[:], AF.Exp, scale=-1.0), reads=[eb.b], writes=[ea.b])


def ffn_phase(K, l, src, dst, final):
    nc, S = K.nc, K.S
    NT = K.NT
    with ExitStack() as es:
        w1 = sbt(K, es, "w1", [128, 8, 2 * FH], BF16, 8)
        w2 = sbt(K, es, "w2", [128, 22, D], BF16, 22)
        gw = sbt(K, es, "gw", [128, D], F32)
        fwt = sbt(K, es, "fwt", [128, D], F32) if final else None
        xt = [sbt(K, es, f"xt{i}", [128, D], F32) for i in range(3)]
        ss = [sbt(K, es, f"ss{i}", [128, 2], F32) for i in range(2)]
        junk = sbt(K, es, "junk", [128, D], BF16)
        junk2 = junk
        xs = sbt(K, es, "xs", [128, D], BF16)
        xT = [sbt(K, es, f"xT{i}", [128, 8, 128], BF16) for i in range(2)]
        ea = [sbt(K, es, f"ea{i}", [128, 512], F32) for i in range(2)]
        eb = [sbt(K, es, f"eb{i}", [128, 512], F32) for i in range(2)]
        hh2 = [sbt(K, es, f"h{i}", [128, FH], BF16) for i in range(2)]
        hT = sbt(K, es, "hT", [128, 22, 128], BF16)
        xo = [sbt(K, es, f"xo{i}", [128, D], F32) for i in range(2)]
        xf = [sbt(K, es, f"xf{i}", [128, D], F32) for i in range(2)] if final else None
        ss2 = sbt(K, es, "ss2", [128, 2], F32)

        S.dma("sp", gw.t[:], K.norm_ffn_w[l].partition_broadcast(128), writes=[gw.b])
        if final:
            S.dma("sp", fwt.t[:], K.final_norm_w.partition_broadcast(128), writes=[fwt.b])
        load_w(K, w1, K.ffn_w_in[l], 8, 1)
        load_w(K, w2, K.ffn_w_out[l], 22, 4)

        def prologue(t):
            sl = t % 2
            x3 = xt[t % 3]
            S.dma("sp", x3.t[:], src[t * 128:(t + 1) * 128, :], writes=[x3.b])
            norm_xs(K, x3, gw, junk, ss[sl], xs)
            transpose8(K, xs, 4, xT[sl].t[:], [xT[sl].b], "act")

        def chunk(t, j):
            sl = t % 2
            h = hh2[t % 2]
            wd = 512 if j < 5 else 256
            p = j % 2
            G, U = bank(K, 2 * p)[:, 0:wd], bank(K, 2 * p + 1)[:, 0:wd]
            for which, pb_i, off in ((G, 2 * p, 0), (U, 2 * p + 1, FH)):
                for k in range(8):
                    S.op("pe", lambda e, k=k, which=which, off=off: e.matmul(
                        which, xT[sl].t[:, k, :], w1.t[:, k, off + j * 512: off + j * 512 + wd],
                        start=(k == 0), stop=(k == 7)),
                        reads=[xT[sl].b, w1.bs[k]], writes=[K.pb[pb_i]])
            A, Bq = ea[p], eb[p]
            S.op("act", lambda e: e.activation(A.t[:, 0:wd], G, AF.Exp, scale=-1.0), reads=[K.pb[2 * p]], writes=[A.b])
            S.op("act", lambda e: e.activation(Bq.t[:, 0:wd], A.t[:, 0:wd], AF.Ln, bias=K.epsb[1.0].t[:, 0:1], scale=1.0), reads=[A.b, K.epsb[1.0].b], writes=[Bq.b])
            S.op("act", lambda e: e.activation(A.t[:, 0:wd], Bq.t[:, 0:wd], AF.Exp, scale=-1.0), reads=[Bq.b], writes=[A.b])
            S.op("dve", lambda e: e.tensor_tensor(Bq.t[:, 0:wd], G, A.t[:, 0:wd], ALU.mult), reads=[K.pb[2 * p], A.b], writes=[Bq.b])
            S.op("dve", lambda e: e.tensor_tensor(h.t[:, j * 512:j * 512 + wd], Bq.t[:, 0:wd], U, ALU.mult), reads=[Bq.b, K.pb[2 * p + 1]], writes=[h.b])

        def stage_T(t):
            h = hh2[t % 2]
            pT = K.ps[:, 5 * 512:8 * 512].bitcast(BF16)
            for f in range(22):
                bi = 5 + (f * 128) // 1024
                S.op("pe", lambda e, f=f: e.transpose(pT[:, f * 128:(f + 1) * 128], h.t[:, f * 128:(f + 1) * 128], K.idb.t[:]),
                     reads=[h.b, K.idb.b], writes=[K.pb[bi]])
            hTf = hT.t[:].rearrange("p f t -> p (f t)")
            S.op("act", lambda e: e.copy(hTf[:, 0:1024], pT[:, 0:1024]), reads=[K.pb[5]], writes=[hT.b])
            S.op("dve", lambda e: e.tensor_copy(hTf[:, 1024:2048], pT[:, 1024:2048]), reads=[K.pb[6]], writes=[hT.b])
            S.op("act", lambda e: e.copy(hTf[:, 2048:2816], pT[:, 2048:2816]), reads=[K.pb[7]], writes=[hT.b])

        def stage_O(t):
            sl = t % 2
            x3 = xt[t % 3]
            for half in range(2):
                for f in range(22):
                    S.op("pe", lambda e, f=f, half=half: e.matmul(
                        bank(K, 5 + half), hT.t[:, f, :], w2.t[:, f, half * 512:(half + 1) * 512],
                        start=(f == 0), stop=(f == 21)),
                        reads=[hT.b, w2.bs[f]], writes=[K.pb[5 + half]])
            S.op("dve", lambda e: e.tensor_tensor(xo[sl].t[:], bank(K, 5, 2), x3.t[:], ALU.add),
                 reads=[K.pb[5], K.pb[6], x3.b], writes=[xo[sl].b])
            if final:
                S.op("act", lambda e: e.activation(junk2.t[:], xo[sl].t[:], AF.Square, accum_out=ss2.t[:, 0:1]),
                     reads=[xo[sl].b], writes=[junk2.b, ss2.b])
                rstd_ops(K, ss2.t[:, 0:1], ss2.t[:, 1:2], 1, 1.0 / D, NORM_EPS, [ss2.b], [ss2.b])
                S.op("dve", lambda e: e.scalar_tensor_tensor(xf[sl].t[:], xo[sl].t[:], ss2.t[:, 1:2], fwt.t[:], ALU.mult, ALU.mult),
                     reads=[xo[sl].b, ss2.b, fwt.b], writes=[xf[sl].b])
                S.dma("pool", dst[t * 128:(t + 1) * 128, :], xf[sl].t[:], reads=[xf[sl].b], writes=[K.xbuf[t]], ring="st")
            else:
                S.dma("pool", dst[t * 128:(t + 1) * 128, :], xo[sl].t[:], reads=[xo[sl].b], writes=[K.xbuf[t]], ring="st")

        prologue(0)
        for t in range(NT + 1):
            if t < NT:
                chunk(t, 0)
            if t >= 1:
                stage_T(t - 1)
            if t < NT:
                chunk(t, 1)
            if t >= 1:
                stage_O(t - 1)
            if t < NT:
                chunk(t, 2)
                if t + 1 < NT:
                    prologue(t + 1)
                chunk(t, 3)
                chunk(t, 4)
                chunk(t, 5)
        S.barrier()


def attn_a1(K, l, src):
    nc, S = K.nc, K.S
    j = l // 2
    SL, NSEQ = K.SL, K.NSEQ
    win = K.attn_w_in[j]
    with ExitStack() as es:
        wq = sbt(K, es, "wq", [128, 8, D], BF16, 8)
        wk = sbt(K, es, "wk", [128, 8, D], BF16, 8)
        wv = sbt(K, es, "wv", [128, 8, D], BF16, 8)
        wqr = sbt(K, es, "wqr", [128, 8, D], BF16)
        wkr = sbt(K, es, "wkr", [128, 8, D], BF16)
        gw = sbt(K, es, "gw", [128, D], F32)
        cosT = sbt(K, es, "cosT", [128, SL], F32)
        sinT = sbt(K, es, "sinT", [128, SL], F32)
        S.dma("sp", gw.t[:], K.norm_mix_w[l].partition_broadcast(128), writes=[gw.b])
        load_w(K, wq, win[:, 0:D], 8, 2)
        load_w(K, wk, win[:, D:2 * D], 8, 2)
        load_w(K, wv, win[:, 2 * D:3 * D], 8, 2)
        for (w, wr) in ((wq, wqr), (wk, wkr)):
            wv5 = w.t[:].rearrange("p k (g two d) -> p k g two d", two=2, d=32)
            wr5 = wr.t[:].rearrange("p k (g two d) -> p k g two d", two=2, d=32)
            for k in range(8):
                eng = "pool" if k % 2 == 0 else "dve"
                S.op(eng, lambda e, k=k, wr5=wr5, wv5=wv5: e.tensor_copy(wr5[:, k, :, 0, :], wv5[:, k, :, 1, :]), reads=[w.bs[k]], writes=[wr.b])
                S.op(eng, lambda e, k=k, wr5=wr5, wv5=wv5: e.tensor_copy(wr5[:, k, :, 1, :], wv5[:, k, :, 0, :]), reads=[w.bs[k]], writes=[wr.b])
        for s in range(NSEQ):
            with ExitStack() as es2:
                posi = sbt(K, es2, "posi", [128, SL], I32)
                u = sbt(K, es2, "u", [128, SL], F32)
                ni = sbt(K, es2, "ni", [128, SL], I32)
                S.dma("sp", posi.t[:], K.pos[s].partition_broadcast(128), writes=[posi.b])
                for (tab, ph) in ((sinT, 0.5), (cosT, 0.75)):
                    S.op("dve", lambda e, ph=ph: e.tensor_scalar(u.t[:], posi.t[:], K.cst.t[:, 0:1], ph, ALU.mult, ALU.add),
                         reads=[posi.b, K.cst.b], writes=[u.b])
                    S.op("dve", lambda e: e.tensor_copy(ni.t[:], u.t[:]), reads=[u.b], writes=[ni.b])
                    S.op("dve", lambda e: e.tensor_tensor(u.t[:], u.t[:], ni.t[:], ALU.subtract), reads=[u.b, ni.b], writes=[u.b])
                    S.op("dve", lambda e: e.tensor_scalar(u.t[:], u.t[:], -0.5, None, ALU.add), reads=[u.b], writes=[u.b])
                    S.op("dve", lambda e: e.scalar_tensor_tensor(u.t[:], u.t[:], -0.5, u.t[:], ALU.is_lt, ALU.add), reads=[u.b], writes=[u.b])
                    S.op("act", lambda e, tab=tab: e.activation(tab.t[:], u.t[:], AF.Sin, scale=2.0 * math.pi), reads=[u.b], writes=[tab.b])
                S.op("dve", lambda e: e.tensor_scalar(sinT.t[:], sinT.t[:], K.cst.t[:, 1:2], None, ALU.mult), reads=[sinT.b, K.cst.b], writes=[sinT.b])
                S.barrier()
            with ExitStack() as es3:
                xt = [sbt(K, es3, f"xt{i}", [128, D], F32) for i in range(2)]
                ss = [sbt(K, es3, f"ss{i}", [128, 2], F32) for i in range(2)]
                junk = sbt(K, es3, "junk", [128, D], BF16)
                xs = sbt(K, es3, "xs", [128, D], BF16)
                xT = sbt(K, es3, "xT", [128, 8, 512], BF16)
                t1 = [sbt(K, es3, f"t1{i}", [128, 512], F32) for i in range(2)]
                t2 = [sbt(K, es3, f"t2{i}", [128, 512], F32) for i in range(2)]
                ob = [sbt(K, es3, f"ob{i}", [128, 8, 512], BF16) for i in range(2)]
                vb = [sbt(K, es3, f"vb{i}", [128, 8, 129], BF16) for i in range(2)]
                for i in range(2):
                    S.op("pool", lambda e, i=i: e.memset(vb[i].t[:], 1.0), writes=[vb[i].b])
                cnt = 0
                vcnt = 0
                for tb in range(SL // 512):
                    for i in range(4):
                        sl = i % 2
                        r0 = s * SL + tb * 512 + i * 128
                        S.dma("sp", xt[sl].t[:], src[r0:r0 + 128, :], writes=[xt[sl].b])
                        norm_xs(K, xt[sl], gw, junk, ss[sl], xs)
                        transpose8(K, xs, 4, xT.t[:, :, i * 128:(i + 1) * 128], [xT.b], "act")
                    for qi, (w, wr, dram) in enumerate(((wq, wqr, K.QT), (wk, wkr, K.KT))):
                        o_ = ob[qi]
                        for hh in range(NH):
                            p = cnt % 2
                            cnt += 1
                            for (ww, pbi) in ((w, 2 * p), (wr, 2 * p + 1)):
                                for k in range(8):
                                    S.op("pe", lambda e, k=k, ww=ww, pbi=pbi: e.matmul(
                                        bank(K, pbi), ww.t[:, k, hh * 128:(hh + 1) * 128], xT.t[:, k, :],
                                        start=(k == 0), stop=(k == 7)),
                                        reads=[xT.b] + ww.bs, writes=[K.pb[pbi]])
                            cs = slice(tb * 512, (tb + 1) * 512)
                            S.op("dve", lambda e: e.tensor_tensor(t1[p].t[:], bank(K, 2 * p), cosT.t[:, cs], ALU.mult),
                                 reads=[K.pb[2 * p], cosT.b], writes=[t1[p].b])
                            S.op("dve", lambda e: e.tensor_tensor(t2[p].t[:], bank(K, 2 * p + 1), sinT.t[:, cs], ALU.mult),
                                 reads=[K.pb[2 * p + 1], sinT.b], writes=[t2[p].b])
                            S.op("pool", lambda e: e.tensor_tensor(o_.t[:, hh, :], t1[p].t[:], t2[p].t[:], ALU.add),
                                 reads=[t1[p].b, t2[p].b], writes=[o_.b])
                        S.dma("pool", dram[s].rearrange("h p s -> p h s")[:, :, tb * 512:(tb + 1) * 512], o_.t[:],
                              reads=[o_.b], writes=[K.scrb[(qi, s, tb)]], ring="st")
                    for i in range(4):
                        v_ = vb[vcnt % 2]
                        vcnt += 1
                        for half in range(2):
                            for k in range(8):
                                S.op("pe", lambda e, k=k, half=half: e.matmul(
                                    bank(K, 5 + half), xT.t[:, k, i * 128:(i + 1) * 128], wv.t[:, k, half * 512:(half + 1) * 512],
                                    start=(k == 0), stop=(k == 7)),
                                    reads=[xT.b] + wv.bs, writes=[K.pb[5 + half]])
                        S.op("act", lambda e: e.copy(v_.t[:, :, 0:128], bank(K, 5, 2).rearrange("p (h v) -> p h v", h=8)),
                             reads=[K.pb[5], K.pb[6]], writes=[v_.b])
                        r0 = s * SL + tb * 512 + i * 128
                        S.dma("pool", K.VA[r0:r0 + 128, :], v_.t[:].rearrange("p h v -> p (h v)"),
                              reads=[v_.b], writes=[K.scrb[(2, r0 // 128)]], ring="st")
                S.barrier()
        S.barrier()


def attn_a2(K, l, src, dst):
    nc, S = K.nc, K.S
    j = l // 2
    SL, NSEQ, T = K.SL, K.NSEQ, K.T
    lam_init = 0.8 - 0.6 * math.exp(-0.3 * l)
    with ExitStack() as es:
        wo = sbt(K, es, "wo", [128, 8, D], BF16, 8)
        KTs = sbt(K, es, "KTs", [128, 8, SL], BF16, 8)
        VAs = sbt(K, es, "VAs", [128, T, 1032], BF16, T)
        swb = sbt(K, es, "swb", [128, 128], F32)
        lv = sbt(K, es, "lv", [128, 4, 64], F32)
        lsum = sbt(K, es, "lsum", [128, 4], F32)
        lam8 = sbt(K, es, "lam8", [128, 4, 2], F32)
        QTb = [sbt(K, es, f"QTb{i}", [128, 8, 512], BF16) for i in range(2)]
        PT = [sbt(K, es, f"PT{i}", [128, 2, 512], BF16) for i in range(4)]
        xt = [sbt(K, es, f"xt{i}", [128, D], F32) for i in range(2)]
        on = sbt(K, es, "on", [128, 4, D], BF16, 4)
        onT = sbt(K, es, "onT", [128, 8, 128], BF16)
        r8 = sbt(K, es, "r8", [128, 4, 2], F32)
        od = sbt(K, es, "od", [128, 4, 128], F32)
        tm = sbt(K, es, "tm", [128, 4, 128], F32)
        sq4 = sbt(K, es, "sq4", [128, 8], F32)
        xo = [sbt(K, es, f"xo{i}", [128, D], F32) for i in range(2)]

        load_w(K, wo, K.attn_w_out[j], 8, 2)
        S.dma("sp", swb.t[:], K.attn_subln_w[j].partition_broadcast(128), writes=[swb.b])
        S.op("dve", lambda e: e.tensor_scalar(swb.t[:], swb.t[:], 1.0 - lam_init, None, ALU.mult), reads=[swb.b], writes=[swb.b])
        for i, a in enumerate((K.attn_lambda_q1, K.attn_lambda_k1, K.attn_lambda_q2, K.attn_lambda_k2)):
            S.dma("sp", lv.t[:, i, :], a[j].partition_broadcast(128), writes=[lv.b])
        S.op("dve", lambda e: e.tensor_tensor(lv.t[:, 0, :], lv.t[:, 0, :], lv.t[:, 1, :], ALU.mult), reads=[lv.b], writes=[lv.b])
        S.op("dve", lambda e: e.tensor_tensor(lv.t[:, 2, :], lv.t[:, 2, :], lv.t[:, 3, :], ALU.mult), reads=[lv.b], writes=[lv.b])
        S.op("dve", lambda e: e.tensor_reduce(lsum.t[:, 0:1], lv.t[:, 0, :], AX.X, ALU.add), reads=[lv.b], writes=[lsum.b])
        S.op("dve", lambda e: e.tensor_reduce(lsum.t[:, 1:2], lv.t[:, 2, :], AX.X, ALU.add), reads=[lv.b], writes=[lsum.b])
        S.op("act", lambda e: e.activation(lsum.t[:, 0:2], lsum.t[:, 0:2], AF.Exp), reads=[lsum.b], writes=[lsum.b])
        S.op("dve", lambda e: e.tensor_tensor(lsum.t[:, 2:3], lsum.t[:, 1:2], lsum.t[:, 0:1], ALU.subtract), reads=[lsum.b], writes=[lsum.b])
        S.op("dve", lambda e: e.tensor_scalar(lsum.t[:, 2:3], lsum.t[:, 2:3], -lam_init, None, ALU.add), reads=[lsum.b], writes=[lsum.b])
        S.op("dve", lambda e: e.memset(lam8.t[:], 1.0), writes=[lam8.b])
        for jj in range(4):
            S.op("dve", lambda e, jj=jj: e.tensor_copy(lam8.t[:, jj, 1:2], lsum.t[:, 2:3]), reads=[lsum.b], writes=[lam8.b])

        accv = K.ps[:, 2048:4096].rearrange("p (j c e) -> p j c e", j=4, c=2)
        accb = [K.pb[4 + jj] for jj in range(4)]
        pcnt = 0
        for s in range(NSEQ):
            for hh in range(NH):
                S.dma("sp", KTs.t[:, hh, :], K.KT[s, hh], reads=[K.scrb[(1, s, tb)] for tb in range(SL // 512)], writes=[KTs.bs[hh]])
            g4 = max(1, T // 4)
            for t0 in range(0, T, g4):
                S.dma("sp", VAs.t[:, t0:t0 + g4, :],
                      K.VA[s * SL + t0 * 128: s * SL + (t0 + g4) * 128, :].rearrange("(t p) c -> p t c", p=128),
                      reads=[K.scrb[(2, s * T + tt)] for tt in range(t0, t0 + g4)], writes=VAs.bs[t0:t0 + g4])
            units = [(qb, hh, kt) for qb in range(SL // 512) for hh in range(NH) for kt in range(4 * qb + 4)]
            ust = {}

            def emit_S(ui):
                nonlocal pcnt
                qb, hh, kt = units[ui]
                Q = QTb[qb % 2]
                if hh == 0 and kt == 0:
                    S.dma("sp", Q.t[:], K.QT[s].rearrange("h p s -> p h s")[:, :, qb * 512:(qb + 1) * 512], reads=[K.scrb[(0, s, qb)]], writes=[Q.b])
                di = kt - 4 * qb
                qlo = max(0, di) * 128
                p = pcnt % 2
                P_ = PT[pcnt % 4]
                pcnt += 1
                ust[ui] = (P_, di, qlo)
                for c in range(2):
                    S.op("pe", lambda e, c=c: e.matmul(
                        bank(K, 2 * p + c)[:, qlo:512], KTs.t[c * 64:(c + 1) * 64, hh, kt * 128:(kt + 1) * 128],
                        Q.t[c * 64:(c + 1) * 64, hh, qlo:512], start=True, stop=True),
                        reads=[KTs.bs[hh], Q.b], writes=[K.pb[2 * p + c]])
                sv = bank(K, 2 * p, 2).rearrange("p (c q) -> p c q", c=2)
                S.op("act", lambda e: e.activation(P_.t[:, :, qlo:512], sv[:, :, qlo:512], AF.Exp, scale=0.125),
                     reads=[K.pb[2 * p], K.pb[2 * p + 1]], writes=[P_.b])
                if di >= 0:
                    S.op("pool", lambda e: e.tensor_tensor(P_.t[:, :, qlo:qlo + 128], P_.t[:, :, qlo:qlo + 128], K.maskc2.t[:], ALU.mult),
                         reads=[P_.b, K.maskc2.b], writes=[P_.b])

            def emit_AV(ui):
                qb, hh, kt = units[ui]
                P_, di, qlo = ust.pop(ui)
                for c in range(2):
                    for jj in range(max(0, di), 4):
                        S.op("pe", lambda e, c=c, jj=jj: e.matmul(
                            accv[:, jj, c, 0:129], P_.t[:, c, jj * 128:(jj + 1) * 128],
                            VAs.t[:, kt, hh * 129:(hh + 1) * 129],
                            start=(kt == 0 and c == 0), stop=(kt == 4 * qb + jj), skip_group_check=True),
                            reads=[P_.b, VAs.bs[kt]], writes=[accb[jj]])

            def post_head(hh):
                S.op("dve", lambda e: e.reciprocal(r8.t[:], accv[:, :, :, 128]), reads=accb, writes=[r8.b])
                S.op("dve", lambda e: e.tensor_tensor(r8.t[:], r8.t[:], lam8.t[:], ALU.mult), reads=[r8.b, lam8.b], writes=[r8.b])
                S.op("dve", lambda e: e.tensor_tensor(od.t[:], accv[:, :, 0, 0:128], r8.t[:, :, 0:1].to_broadcast([128, 4, 128]), ALU.mult),
                     reads=accb + [r8.b], writes=[od.b])
                S.op("dve", lambda e: e.tensor_tensor(tm.t[:], accv[:, :, 1, 0:128], r8.t[:, :, 1:2].to_broadcast([128, 4, 128]), ALU.mult),
                     reads=accb + [r8.b], writes=[tm.b])
                S.op("pool", lambda e: e.tensor_tensor(od.t[:], od.t[:], tm.t[:], ALU.add), reads=[od.b, tm.b], writes=[od.b])
                S.op("pool", lambda e: e.tensor_tensor(tm.t[:], od.t[:], od.t[:], ALU.mult), reads=[od.b], writes=[tm.b])
                S.op("dve", lambda e: e.tensor_reduce(sq4.t[:, 0:4], tm.t[:], AX.X, ALU.add), reads=[tm.b], writes=[sq4.b])
                rstd_ops(K, sq4.t[:, 0:4], sq4.t[:, 4:8], 4, 1.0 / 128, SUBLN_EPS, [sq4.b], [sq4.b])
                S.op("dve", lambda e: e.tensor_tensor(tm.t[:], od.t[:], sq4.t[:, 4:8].unsqueeze(2).to_broadcast([128, 4, 128]), ALU.mult),
                     reads=[od.b, sq4.b], writes=[tm.b])
                S.op("pool", lambda e: e.tensor_tensor(on.t[:, :, hh * 128:(hh + 1) * 128], tm.t[:],
                                                       swb.t[:].unsqueeze(1).to_broadcast([128, 4, 128]), ALU.mult),
                     reads=[tm.b, swb.b], writes=on.bs)

            def out_proj(qb):
                nonlocal pcnt
                for jj in range(4):
                    r0 = s * SL + qb * 512 + jj * 128
                    S.dma("sp", xt[jj % 2].t[:], src[r0:r0 + 128, :], reads=[K.xbuf[r0 // 128]], writes=[xt[jj % 2].b])
                    p = pcnt % 2
                    pcnt += 1
                    pT = bank(K, 2 * p).bitcast(BF16)
                    for k in range(8):
                        S.op("pe", lambda e, k=k: e.transpose(pT[:, k * 128:(k + 1) * 128], on.t[:, jj, k * 128:(k + 1) * 128], K.idb.t[:]),
                             reads=[on.bs[jj], K.idb.b], writes=[K.pb[2 * p]])
                    S.op("dve", lambda e: e.tensor_copy(onT.t[:], pT[:, 0:1024].rearrange("p (k t) -> p k t", k=8)), reads=[K.pb[2 * p]], writes=[onT.b])
                    p = pcnt % 2
                    pcnt += 1
                    for half in range(2):
                        for k in range(8):
                            S.op("pe", lambda e, k=k, half=half: e.matmul(
                                bank(K, 2 * p + half), onT.t[:, k, :], wo.t[:, k, half * 512:(half + 1) * 512],
                                start=(k == 0), stop=(k == 7)),
                                reads=[onT.b] + wo.bs, writes=[K.pb[2 * p + half]])
                    x_ = xo[jj % 2]
                    S.op("dve", lambda e: e.tensor_tensor(x_.t[:], bank(K, 2 * p, 2), xt[jj % 2].t[:], ALU.add),
                         reads=[K.pb[2 * p], K.pb[2 * p + 1], xt[jj % 2].b], writes=[x_.b])
                    S.dma("pool", dst[r0:r0 + 128, :], x_.t[:], reads=[x_.b], writes=[K.xbuf[r0 // 128]], ring="st")

            emit_S(0)
            for ui, (qb, hh, kt) in enumerate(units):
                if ui + 1 < len(units):
                    emit_S(ui + 1)
                emit_AV(ui)
                if kt == 4 * qb + 3:
                    post_head(hh)
                    if hh == NH - 1:
                        out_proj(qb)
        S.barrier()


def hgrn_phase(K, l, src, dst):
    nc, S = K.nc, K.S
    j = l // 2
    SL, NSEQ, T = K.SL, K.NSEQ, K.T
    with ExitStack() as es:
        lbb = sbt(K, es, "lbb", [128, D], F32)
        with ExitStack() as es0:
            lp = sbt(K, es0, "lp", [128, 4, D], F32)
            den = sbt(K, es0, "den", [128, D], F32)
            for i in range(4):
                S.dma("sp", lp.t[:, i, :], K.hgrn_lb_param[i].partition_broadcast(128), writes=[lp.b])
            S.op("act", lambda e: e.activation(lp.t[:], lp.t[:], AF.Exp), reads=[lp.b], writes=[lp.b])
            S.op("dve", lambda e: e.tensor_tensor(den.t[:], lp.t[:, 0, :], lp.t[:, 1, :], ALU.add), reads=[lp.b], writes=[den.b])
            S.op("dve", lambda e: e.tensor_tensor(den.t[:], den.t[:], lp.t[:, 2, :], ALU.add), reads=[lp.b, den.b], writes=[den.b])
            S.op("dve", lambda e: e.tensor_tensor(den.t[:], den.t[:], lp.t[:, 3, :], ALU.add), reads=[lp.b, den.b], writes=[den.b])
            S.op("dve", lambda e: e.reciprocal(den.t[:], den.t[:]), reads=[den.b], writes=[den.b])
            S.op("dve", lambda e: e.memset(lbb.t[:], 0.0), writes=[lbb.b])
            for i in range(1, l + 1):
                S.op("dve", lambda e, i=i: e.tensor_tensor(lbb.t[:], lbb.t[:], lp.t[:, i, :], ALU.add), reads=[lp.b, lbb.b], writes=[lbb.b])
            S.op("dve", lambda e: e.tensor_tensor(lbb.t[:], lbb.t[:], den.t[:], ALU.mult), reads=[den.b, lbb.b], writes=[lbb.b])
            S.barrier()
        win = sbt(K, es, "win", [128, 8, 4 * D], BF16, 8)
        wout = sbt(K, es, "wout", [128, 8, D], BF16, 8)
        gw = sbt(K, es, "gw", [128, D], F32)
        gnb = sbt(K, es, "gnb", [128, 128], F32)
        xt = [sbt(K, es, f"xt{i}", [128, D], F32) for i in range(2)]
        ss = [sbt(K, es, f"ss{i}", [128, 4], F32) for i in range(2)]
        junk = sbt(K, es, "junk", [128, D], BF16)
        xs = sbt(K, es, "xs", [128, D], BF16)
        xT = sbt(K, es, "xT", [128, 8, 128], BF16)
        E = [sbt(K, es, f"E{i}", [128, D], F32) for i in range(4)]
        Ebg = [[sbt(K, es, f"Eb{i}{g}", [128, 512], F32) for g in range(2)] for i in range(2)]
        sq8g = [sbt(K, es, f"sq8g{g}", [128, 8], F32) for g in range(2)]
        logf = sbt(K, es, "logf", [128, D], F32)
        sgw2 = [sbt(K, es, f"sgw{i}", [128, D], F32) for i in range(2)]
        kd2 = [sbt(K, es, f"kd{i}", [128, D], BF16) for i in range(2)]
        qd2 = [sbt(K, es, f"qd{i}", [128, D], BF16) for i in range(2)]
        vv2 = [sbt(K, es, f"vv{i}", [128, D], BF16) for i in range(2)]
        scr2 = [sbt(K, es, f"scr{i}", [128, 8], F32) for i in range(2)]
        sb2 = [sbt(K, es, f"sb{i}", [128, 8], F32) for i in range(2)]
        onb = sbt(K, es, "onb", [128, D], BF16)
        qdTa = sbt(K, es, "qdTa", [128, 8, 128], BF16)
        qdTb = sbt(K, es, "qdTb", [128, 8, 128], BF16)
        kdT = sbt(K, es, "kdT", [128, 8, 128], BF16)
        Am = sbt(K, es, "Am", [128, 8, 128], BF16, 2)
        onT = sbt(K, es, "onT", [128, 8, 128], BF16)
        W = sbt(K, es, "W", [128, 8, 128], F32, 8)
        R0 = sbt(K, es, "R0", [128, 8, 128], BF16)
        R1 = sbt(K, es, "R1", [128, 8, 128], BF16, 2)
        dcur = sbt(K, es, "dcur", [128, 8, 3], F32)
        cprev = sbt(K, es, "cprev", [128, 8], F32)
        sq8 = sbt(K, es, "sq8", [128, 16], F32)
        xo = [sbt(K, es, f"xo{i}", [128, D], F32) for i in range(2)]

        S.dma("sp", gw.t[:], K.norm_mix_w[l].partition_broadcast(128), writes=[gw.b])
        S.dma("sp", gnb.t[:], K.hgrn_gnorm_w[j].partition_broadcast(128), writes=[gnb.b])
        load_w(K, win, K.hgrn_w_in[j], 8, 1)
        load_w(K, wout, K.hgrn_w_out[j], 8, 2)
        S.op("pool", lambda e: e.memset(qdTa.t[:], 0.0), writes=[qdTa.b])
        S.op("pool", lambda e: e.memset(qdTb.t[:], 0.0), writes=[qdTb.b])
        one = K.epsb[1.0]

        def proj(qi):
            p = next_pair(K)
            for half in range(2):
                for k in range(8):
                    S.op("pe", lambda e, k=k, half=half: e.matmul(
                        bank(K, 2 * p + half), xT.t[:, k, :], win.t[:, k, qi * D + half * 512: qi * D + (half + 1) * 512],
                        start=(k == 0), stop=(k == 7)),
                        reads=[xT.b, win.bs[k]], writes=[K.pb[2 * p + half]])
            return bank(K, 2 * p, 2), [K.pb[2 * p], K.pb[2 * p + 1]]

        def sig_chain(src_ps, src_b, A, Bq, sst):
            S.op("act", lambda e: e.activation(A.t[:], src_ps, AF.Exp, scale=sst.t[:, 2:3]), reads=src_b + [sst.b], writes=[A.b])
            S.op("act", lambda e: e.activation(Bq.t[:], A.t[:], AF.Ln, bias=one.t[:, 0:1], scale=1.0), reads=[A.b, one.b], writes=[Bq.b])
            S.op("act", lambda e: e.activation(A.t[:], Bq.t[:], AF.Exp, scale=-1.0), reads=[Bq.b], writes=[A.b])

        def A1(t):
            sl = t % 2
            tl = t % T
            kd, scr_, sb_ = kd2[sl], scr2[sl], sb2[sl]
            if tl == 0:
                S.op("dve", lambda e: e.memset(cprev.t[:], 0.0), writes=[cprev.b])
            S.dma("sp", xt[sl].t[:], src[t * 128:(t + 1) * 128, :], reads=[K.xbuf[t]], writes=[xt[sl].b])
            S.op("dve", lambda e: e.tensor_tensor(xs.t[:], xt[sl].t[:], gw.t[:], ALU.mult), reads=[xt[sl].b, gw.b], writes=[xs.b])
            S.op("act", lambda e: e.activation(junk.t[:], xt[sl].t[:], AF.Square, accum_out=ss[sl].t[:, 0:1]),
                 reads=[xt[sl].b], writes=[junk.b, ss[sl].b])
            rstd_ops(K, ss[sl].t[:, 0:1], ss[sl].t[:, 1:2], 1, 1.0 / D, NORM_EPS, [ss[sl].b], [ss[sl].b])
            S.op("dve", lambda e: e.tensor_scalar(ss[sl].t[:, 2:3], ss[sl].t[:, 1:2], -1.0, None, ALU.mult), reads=[ss[sl].b], writes=[ss[sl].b])
            p = next_pair(K)
            transpose8(K, xs, 2 * p, xT.t[:], [xT.b], "act")
            fz, fzb = proj(1)
            S.op("act", lambda e: e.activation(E[0].t[:], fz, AF.Exp, scale=ss[sl].t[:, 2:3]), reads=fzb + [ss[sl].b], writes=[E[0].b])
            S.op("dve", lambda e: e.tensor_tensor(E[1].t[:], E[0].t[:], lbb.t[:], ALU.mult), reads=[E[0].b, lbb.b], writes=[E[1].b])
            S.op("act", lambda e: e.activation(E[1].t[:], E[1].t[:], AF.Ln, bias=one.t[:, 0:1], scale=1.0), reads=[E[1].b, one.b], writes=[E[1].b])
            S.op("act", lambda e: e.activation(E[0].t[:], E[0].t[:], AF.Ln, bias=one.t[:, 0:1], scale=1.0), reads=[E[0].b, one.b], writes=[E[0].b])
            S.op("dve", lambda e: e.tensor_tensor(logf.t[:], E[1].t[:], E[0].t[:], ALU.subtract), reads=[E[0].b, E[1].b], writes=[logf.b])
            p = next_pair(K)
            for half in range(2):
                S.op("pe", lambda e, half=half: e.matmul(bank(K, 2 * p + half), K.Tm.t[:], logf.t[:, half * 512:(half + 1) * 512], start=True, stop=True),
                     reads=[K.Tm.b, logf.b], writes=[K.pb[2 * p + half]])
            brel, brb = bank(K, 2 * p, 2), [K.pb[2 * p], K.pb[2 * p + 1]]
            pd = next_pair(K)
            dps = bank(K, 2 * pd)[:, 0:24].rearrange("p (h c) -> p h c", c=3)
            for hh in range(NH):
                S.op("pe", lambda e, hh=hh: e.matmul(dps[:, hh, :], logf.t[:, hh * 128:(hh + 1) * 128], K.Sel.t[:], start=True, stop=True),
                     reads=[logf.b, K.Sel.b], writes=[K.pb[2 * pd]])
            S.op("dve", lambda e: e.tensor_copy(dcur.t[:], dps), reads=[K.pb[2 * pd]], writes=[dcur.b])
            S.op("act", lambda e: e.activation(E[2].t[:], brel, AF.Exp), reads=brb, writes=[E[2].b])
            S.op("act", lambda e: e.activation(E[3].t[:], brel, AF.Exp, scale=-1.0), reads=brb, writes=[E[3].b])
            S.op("act", lambda e: e.activation(E[0].t[:], logf.t[:], AF.Exp), reads=[logf.b], writes=[E[0].b])
            S.op("dve", lambda e: e.tensor_scalar(E[0].t[:], E[0].t[:], -1.0, 1.0, ALU.mult, ALU.add), reads=[E[0].b], writes=[E[0].b])
            S.op("dve", lambda e: e.tensor_tensor(kd.t[:], E[0].t[:], E[3].t[:], ALU.mult), reads=[E[0].b, E[3].b], writes=[kd.b])
            S.op("dve", lambda e: e.tensor_tensor(scr_.t[:], cprev.t[:], dcur.t[:, :, 0], ALU.add), reads=[cprev.b, dcur.b], writes=[scr_.b])
            S.op("act", lambda e: e.activation(scr_.t[:], scr_.t[:], AF.Exp), reads=[scr_.b], writes=[scr_.b])
            S.op("act", lambda e: e.activation(sb_.t[:], dcur.t[:, :, 1], AF.Exp), reads=[dcur.b], writes=[sb_.b])
            S.op("dve", lambda e: e.tensor_copy(cprev.t[:], dcur.t[:, :, 2]), reads=[dcur.b], writes=[cprev.b])

        def A2(t):
            sl = t % 2
            qd = qd2[sl]
            q_, qb_ = proj(0)
            sig_chain(q_, qb_, E[0], E[1], ss[sl])
            S.op("dve", lambda e: e.scalar_tensor_tensor(E[1].t[:], q_, ss[sl].t[:, 1:2], E[0].t[:], ALU.mult, ALU.mult), reads=qb_ + [E[0].b, ss[sl].b], writes=[E[1].b])
            S.op("dve", lambda e: e.tensor_tensor(qd.t[:], E[1].t[:], E[2].t[:], ALU.mult), reads=[E[1].b, E[2].b], writes=[qd.b])

        def A3(t):
            sl = t % 2
            vv, sgw = vv2[sl], sgw2[sl]
            i_, ib_ = proj(2)
            S.op("act", lambda e: e.mul(vv.t[:], i_, ss[sl].t[:, 1:2]), reads=ib_ + [ss[sl].b], writes=[vv.b])
            g_, gb_ = proj(3)
            sig_chain(g_, gb_, E[0], E[1], ss[sl])
            S.op("dve", lambda e: e.scalar_tensor_tensor(E[1].t[:], g_, ss[sl].t[:, 1:2], E[0].t[:], ALU.mult, ALU.mult), reads=gb_ + [E[0].b, ss[sl].b], writes=[E[1].b])
            S.op("pool", lambda e: e.tensor_tensor(sgw.t[:].rearrange("p (h v) -> p h v", h=8), E[1].t[:].rearrange("p (h v) -> p h v", h=8),
                                                   gnb.t[:].unsqueeze(1).to_broadcast([128, 8, 128]), ALU.mult),
                 reads=[E[1].b, gnb.b], writes=[sgw.b])

        at = bank(K, 5).rearrange("p (h c) -> p h c", h=4)
        m0 = bank(K, 6).rearrange("p (h c) -> p h c", h=4)
        m1 = bank(K, 7).rearrange("p (h c) -> p h c", h=4)
        bst = {}

        def B0(t):
            sl = t % 2
            tl = t % T
            kd, qd, scr_ = kd2[sl], qd2[sl], scr2[sl]
            if tl == 0:
                S.op("dve", lambda e: e.memset(W.t[:], 0.0), writes=W.bs)
            S.op("pool", lambda e: e.tensor_tensor(R0.t[:], W.t[:], scr_.t[:].unsqueeze(2).to_broadcast([128, 8, 128]), ALU.mult),
                 reads=W.bs + [scr_.b], writes=[R0.b])
            p = next_pair(K)
            pT = bank(K, 2 * p).bitcast(BF16)
            for k in range(8):
                S.op("pe", lambda e, k=k: e.transpose(pT[:, k * 128:(k + 1) * 128], qd.t[:, k * 128:(k + 1) * 128], K.idb.t[:]),
                     reads=[qd.b, K.idb.b], writes=[K.pb[2 * p]])
            pv = pT[:, 0:1024].rearrange("p (k t) -> p k t", k=8)
            S.op("act", lambda e: e.copy(qdTa.t[:, :, 0:64], pv[:, :, 0:64]), reads=[K.pb[2 * p]], writes=[qdTa.b])
            S.op("dve", lambda e: e.tensor_copy(qdTb.t[:, :, 64:128], pv[:, :, 64:128]), reads=[K.pb[2 * p]], writes=[qdTb.b])
            p = next_pair(K)
            transpose8(K, kd, 2 * p, kdT.t[:], [kdT.b], "dve")

        def Bg_mm(t, hg):
            sl = t % 2
            kd, vv = kd2[sl], vv2[sl]
            hs = range(4 * hg, 4 * hg + 4)
            for hh in hs:
                S.op("pe", lambda e, hh=hh: e.matmul(at[:, hh % 4, 0:64], kdT.t[:, hh, :], qdTa.t[:, hh, 0:64], start=True, stop=True),
                     reads=[kdT.b, qdTa.b], writes=[K.pb[5]])
                S.op("pe", lambda e, hh=hh: e.matmul(at[:, hh % 4, 64:128], kdT.t[:, hh, :], qdTb.t[:, hh, 64:128], start=True, stop=True),
                     reads=[kdT.b, qdTb.b], writes=[K.pb[5]])
            for hh in hs:
                S.op("pe", lambda e, hh=hh: e.matmul(m0[:, hh % 4, :], kd.t[0:64, hh * 128:(hh + 1) * 128], vv.t[0:64, hh * 128:(hh + 1) * 128], start=True, stop=True),
                     reads=[kd.b, vv.b], writes=[K.pb[6]])
            for hh in hs:
                S.op("pe", lambda e, hh=hh: e.matmul(m1[:, hh % 4, :], kd.t[64:128, hh * 128:(hh + 1) * 128], vv.t[64:128, hh * 128:(hh + 1) * 128], start=True, stop=True),
                     reads=[kd.b, vv.b], writes=[K.pb[7]])
            S.op("dve", lambda e: e.tensor_tensor(Am.t[:, 4 * hg:4 * hg + 4, :], at, K.mask64.t[:].unsqueeze(1).to_broadcast([128, 4, 128]), ALU.mult),
                 reads=[K.pb[5], K.mask64.b], writes=[Am.bs[hg]])

        def Bg_state(t, hg):
            sl = t % 2
            scr_, sb_, vv = scr2[sl], sb2[sl], vv2[sl]
            o4 = bank(K, 4).rearrange("p (h v) -> p h v", h=4)
            hs = range(4 * hg, 4 * hg + 4)
            for hh in hs:
                S.op("dve", lambda e, hh=hh: e.scalar_tensor_tensor(W.t[:, hh, :], W.t[:, hh, :], scr_.t[:, hh:hh + 1], m0[:, hh % 4, :], ALU.mult, ALU.add),
                     reads=[W.bs[hh], scr_.b, K.pb[6], R0.b], writes=[W.bs[hh]])
            S.op("pool", lambda e: e.tensor_tensor(R1.t[:, 4 * hg:4 * hg + 4, :], W.t[:, 4 * hg:4 * hg + 4, :],
                                                   sb_.t[:, 4 * hg:4 * hg + 4].unsqueeze(2).to_broadcast([128, 4, 128]), ALU.mult),
                 reads=W.bs[4 * hg:4 * hg + 4] + [sb_.b], writes=[R1.bs[hg]])
            for hh in hs:
                oh = o4[:, hh % 4, :]
                pbo = K.pb[4]
                S.op("pe", lambda e, hh=hh, oh=oh: e.matmul(oh, Am.t[:, hh, :], vv.t[:, hh * 128:(hh + 1) * 128], start=True, stop=False),
                     reads=[Am.bs[hg], vv.b], writes=[pbo])
                S.op("pe", lambda e, hh=hh, oh=oh: e.matmul(oh, qdTa.t[:, hh, :], R0.t[:, hh, :], start=False, stop=False),
                     reads=[qdTa.b, R0.b], writes=[pbo])
                S.op("pe", lambda e, hh=hh, oh=oh: e.matmul(oh, qdTb.t[:, hh, :], R1.t[:, hh, :], start=False, stop=True),
                     reads=[qdTb.b, R1.bs[hg]], writes=[pbo])
            for hh in hs:
                S.op("dve", lambda e, hh=hh: e.scalar_tensor_tensor(W.t[:, hh, :], W.t[:, hh, :], sb_.t[:, hh:hh + 1], m1[:, hh % 4, :], ALU.mult, ALU.add),
                     reads=[W.bs[hh], sb_.b, K.pb[7], R1.bs[hg]], writes=[W.bs[hh]])
            sgw = sgw2[sl]
            cs = slice(512 * hg, 512 * hg + 512)
            e0, e1, sq = Ebg[0][hg], Ebg[1][hg], sq8g[hg]
            S.op("act", lambda e: e.activation(e0.t[:], bank(K, 4), AF.Square), reads=[K.pb[4]], writes=[e0.b])
            S.op("dve", lambda e: e.tensor_reduce(sq.t[:, 0:4], e0.t[:].rearrange("p (h v) -> p h v", h=4), AX.X, ALU.add), reads=[e0.b], writes=[sq.b])
            rstd_ops(K, sq.t[:, 0:4], sq.t[:, 4:8], 4, 1.0 / 128, GN_EPS, [sq.b], [sq.b])
            S.op("dve", lambda e: e.tensor_tensor(e1.t[:].rearrange("p (h v) -> p h v", h=4), o4,
                                                  sq.t[:, 4:8].unsqueeze(2).to_broadcast([128, 4, 128]), ALU.mult),
                 reads=[K.pb[4], sq.b], writes=[e1.b])
            S.op("pool", lambda e: e.tensor_tensor(onb.t[:, cs], e1.t[:], sgw.t[:, cs], ALU.mult), reads=[e1.b, sgw.b], writes=[onb.b])

        def B3(t):
            sl = t % 2
            p = next_pair(K)
            transpose8(K, onb, 2 * p, onT.t[:], [onT.b], "act")
            p = next_pair(K)
            for half in range(2):
                for k in range(8):
                    S.op("pe", lambda e, k=k, half=half: e.matmul(
                        bank(K, 2 * p + half), onT.t[:, k, :], wout.t[:, k, half * 512:(half + 1) * 512],
                        start=(k == 0), stop=(k == 7)),
                        reads=[onT.b] + wout.bs, writes=[K.pb[2 * p + half]])
            S.op("dve", lambda e: e.tensor_tensor(xo[sl].t[:], bank(K, 2 * p, 2), xt[sl].t[:], ALU.add),
                 reads=[K.pb[2 * p], K.pb[2 * p + 1], xt[sl].b], writes=[xo[sl].b])
            S.dma("pool", dst[t * 128:(t + 1) * 128, :], xo[sl].t[:], reads=[xo[sl].b], writes=[K.xbuf[t]], ring="st")

        NTT = NSEQ * T
        A1(0); A2(0); A3(0)
        for t in range(NTT):
            nx = t + 1 < NTT
            B0(t)
            Bg_mm(t, 0)
            if nx:
                A1(t + 1)
            Bg_state(t, 0)
            Bg_mm(t, 1)
            if nx:
                A2(t + 1)
            Bg_state(t, 1)
            B3(t)
            if nx:
                A3(t + 1)
        S.barrier()


def host_consts():
    ident = np.eye(128, dtype=np.float32)
    kk = np.arange(128)[:, None]
    qq = np.arange(128)[None, :]
    maskc = (kk <= qq).astype(np.float32)
    mask64 = ((kk <= qq) & ((kk // 64) == (qq // 64))).astype(np.float32)
    Tm = np.zeros((128, 128), np.float32)
    for c in range(128):
        base = (c // 64) * 64
        ref = base + 31
        if c > ref:
            Tm[ref + 1:c + 1, c] = 1.0
        elif c < ref:
            Tm[c + 1:ref + 1, c] = -1.0
    Sel = np.zeros((128, 3), np.float32)
    Sel[0:32, 0] = 1.0
    Sel[32:96, 1] = 1.0
    Sel[96:128, 2] = 1.0
    p = np.arange(128)
    invf = 1.0 / (10000.0 ** ((p % 32).astype(np.float64) / 32.0))
    cst = np.zeros((128, 4), np.float32)
    cst[:, 0] = (invf / (2.0 * np.pi)).astype(np.float32)
    cst[:, 1] = np.where((p % 64) < 32, -1.0, 1.0)
    cst[:, 2] = 1.0
    return dict(c_ident=ident, c_maskc=maskc, c_mask64=mask64, c_Tm=Tm, c_Sel=Sel, c_cst=cst)


WSPECS = [
    ("norm_mix_w", [4, D]), ("norm_ffn_w", [4, D]), ("final_norm_w", [D]),
    ("attn_w_in", [2, D, 3 * D]), ("attn_w_out", [2, D, D]),
    ("attn_lambda_q1", [2, 64]), ("attn_lambda_k1", [2, 64]), ("attn_lambda_q2", [2, 64]), ("attn_lambda_k2", [2, 64]),
    ("attn_subln_w", [2, 128]), ("hgrn_w_in", [2, D, 4 * D]), ("hgrn_w_out", [2, D, D]),
    ("hgrn_gnorm_w", [2, 128]), ("hgrn_lb_param", [4, D]),
    ("ffn_w_in", [4, D, 2 * FH]), ("ffn_w_out", [4, FH, D]),
]
CSPECS = [("c_ident", [128, 128]), ("c_maskc", [128, 128]), ("c_mask64", [128, 128]), ("c_Tm", [128, 128]),
          ("c_Sel", [128, 3]), ("c_cst", [128, 4])]


def build(NSEQ=2, SL=4096, prog=None):
    if prog is None:
        prog = []
        for l in range(4):
            prog.append(("attn" if l % 2 == 0 else "hgrn", l))
            prog.append(("ffn", l))
    nc = bass.Bass("TRN2", target_bir_lowering=False)
    K = Ctx()
    K.nc = nc
    K.uid = 0
    K.NSEQ, K.SL = NSEQ, SL
    K.T = SL // 128
    K.NT = NSEQ * K.T
    NTOK = NSEQ * SL
    x_in = nc.dram_tensor("x", [NTOK, D], F32, kind="ExternalInput").ap()
    K.pos = nc.dram_tensor("positions", [NSEQ, SL], I32, kind="ExternalInput").ap()
    for name, shp in WSPECS:
        setattr(K, name, nc.dram_tensor(name, shp, F32, kind="ExternalInput").ap())
    cd = {name: nc.dram_tensor(name, shp, F32, kind="ExternalInput").ap() for name, shp in CSPECS}
    out = nc.dram_tensor("out", [NTOK, D], F32, kind="ExternalOutput").ap()
    xres = nc.dram_tensor("xres", [NTOK, D], F32).ap()
    K.QT = nc.dram_tensor("QT", [NSEQ, NH, 128, SL], BF16).ap()
    K.KT = nc.dram_tensor("KT", [NSEQ, NH, 128, SL], BF16).ap()
    K.VA = nc.dram_tensor("VA", [NTOK, 1032], BF16).ap()
    with ExitStack() as es:
        S = Sched(nc, es)
        K.S = S
        K.ps = es.enter_context(nc.psum_tensor("ps", [128, 4096], F32))
        K.pb = [Buf(f"pb{i}") for i in range(8)]
        K.pair_i = 0
        K.xbuf = [Buf() for _ in range(K.NT)]
        K.scrb = {}
        for s_ in range(NSEQ):
            for tb_ in range(SL // 512):
                K.scrb[(0, s_, tb_)] = Buf()
                K.scrb[(1, s_, tb_)] = Buf()
        for t_ in range(K.NT):
            K.scrb[(2, t_)] = Buf()
        K.idb = sbt(K, es, "idb", [128, 128], BF16)
        K.maskc2 = sbt(K, es, "maskc2", [128, 2, 128], BF16)
        K.mask64 = sbt(K, es, "mask64", [128, 128], F32)
        K.Tm = sbt(K, es, "Tm", [128, 128], F32)
        K.Sel = sbt(K, es, "Sel", [128, 3], F32)
        K.cst = sbt(K, es, "cst", [128, 4], F32)
        K.epsb = {}
        for v in (NORM_EPS, SUBLN_EPS, 1.0):
            if v not in K.epsb:
                tl_ = sbt(K, es, f"eps{len(K.epsb)}", [128, 1], F32)
                S.op("dve", lambda e, tl_=tl_, v=v: e.memset(tl_.t[:], v), writes=[tl_.b])
                K.epsb[v] = tl_
        with ExitStack() as es0:
            tmpf = sbt(K, es0, "tmpf", [128, 128], F32)
            S.dma("sp", tmpf.t[:], cd["c_ident"], writes=[tmpf.b])
            S.op("dve", lambda e: e.tensor_copy(K.idb.t[:], tmpf.t[:]), reads=[tmpf.b], writes=[K.idb.b])
            tmpm = sbt(K, es0, "tmpm", [128, 128], F32)
            S.dma("sp", tmpm.t[:], cd["c_maskc"], writes=[tmpm.b])
            for c in range(2):
                S.op("dve", lambda e, c=c: e.tensor_copy(K.maskc2.t[:, c, :], tmpm.t[:]), reads=[tmpm.b], writes=[K.maskc2.b])
            S.dma("sp", K.mask64.t[:], cd["c_mask64"], writes=[K.mask64.b])
            S.dma("sp", K.Tm.t[:], cd["c_Tm"], writes=[K.Tm.b])
            S.dma("sp", K.Sel.t[:], cd["c_Sel"], writes=[K.Sel.b])
            S.dma("sp", K.cst.t[:], cd["c_cst"], writes=[K.cst.b])
            S.barrier()
        cur = x_in
        nph = len(prog)
        for pi, (kind, l) in enumerate(prog):
            last = pi == nph - 1
            dst = out if last else xres
            if kind == "ffn":
                ffn_phase(K, l, cur, dst, final=last)
            elif kind == "attn":
                attn_a1(K, l, cur)
                attn_a2(K, l, cur, dst)
            elif kind == "hgrn":
                hgrn_phase(K, l, cur, dst)
            cur = dst
        S.barrier()
        K.stats = (S.nsem, dict(S.n_inst), dict(S.n_wait))
    return nc, K


_CACHE = {}


def kernel(**inputs):
    x = np.asarray(inputs["x"], np.float32)
    pos = np.asarray(inputs["positions"], np.int32)
    B, SLn, _ = x.shape
    ncore = 8
    nseq = B // ncore
    key = (nseq, SLn)
    if key not in _CACHE:
        _CACHE[key] = build(nseq, SLn)[0]
    nc = _CACHE[key]
    consts = host_consts()
    wts = {name: np.ascontiguousarray(np.asarray(inputs[name], np.float32)) for name, _ in WSPECS}
    in_maps = []
    for c in range(ncore):
        m = {"x": np.ascontiguousarray(x[c * nseq:(c + 1) * nseq].reshape(nseq * SLn, D)),
             "positions": np.ascontiguousarray(pos[c * nseq:(c + 1) * nseq])}
        m.update(wts)
        m.update(consts)
        in_maps.append(m)
    res = run_bass_kernel_spmd(nc, in_maps, core_ids=list(range(ncore)))
    outs = [np.asarray(r["out"], np.float32).reshape(nseq, SLn, D) for r in res.results]
    return np.concatenate(outs, axis=0)
```
